# Optimizing a Trainium2 kernel written in Bass

```python
import math
import jax, jax.numpy as jnp
from jax import lax
import numpy as np

D_MODEL = 1024
BATCH = 8
SEQ = 2048
DEPTH = 2

GRID_W = 64
CTX_LEN = 256
N_HEADS_A = 4
DH_A = 64
DV_A = 2 * DH_A
WIDTH_A = N_HEADS_A * DV_A
N_HEADS_B = 4
DK_B = 64
DV_B = 64
WIDTH_BK = N_HEADS_B * DK_B
WIDTH_B = N_HEADS_B * DV_B
FN_GROUPS = 4
FN_GROUP_DIM = 64
WIDTH_C = FN_GROUPS * FN_GROUP_DIM
MIX_WIDTH = WIDTH_A + WIDTH_B + WIDTH_C
PROJ_SIZES = (WIDTH_A, WIDTH_A, WIDTH_A, WIDTH_BK, WIDTH_BK, WIDTH_BK, WIDTH_B, WIDTH_B, WIDTH_C)
PROJ_WIDTH = 3 * WIDTH_A + 3 * WIDTH_BK + 2 * WIDTH_B + WIDTH_C
D_FF = 2816
CONV_W = 3
Q_BLOCK = 128
CHUNK = 64
ROPE_BASE = 10000.0
EPS = 1e-6

kernel_name = 'hybrid_diffattn_hgrn2_fnet_convffn_dit'


def rms_norm(x, w):
    xf = x.astype(jnp.float32)
    y = xf * lax.rsqrt(jnp.mean(xf * xf, axis=-1, keepdims=True) + EPS)
    return (y * w.astype(jnp.float32)).astype(x.dtype)


def modulate(h, shift, scale):
    return h * (1 + scale) + shift


def flip(a):
    return jnp.flip(a, axis=1)


def axial_rope(rows):
    pos_r = jnp.repeat(jnp.arange(rows), GRID_W).astype(jnp.float32)
    pos_c = jnp.tile(jnp.arange(GRID_W), rows).astype(jnp.float32)
    half = DH_A // 2
    inv_freq = ROPE_BASE ** (-jnp.arange(0, half, 2, dtype=jnp.float32) / half)
    ang_r = pos_r[:, None] * inv_freq
    ang_c = pos_c[:, None] * inv_freq
    ang = jnp.concatenate([ang_r, ang_r, ang_c, ang_c], axis=-1)
    return jnp.cos(ang), jnp.sin(ang)


def _rotate_half(p):
    h = p.shape[-1] // 2
    return jnp.concatenate([-p[..., h:], p[..., :h]], axis=-1)


def apply_axial_rope(t, cos, sin):
    t_row, t_col = jnp.split(t, 2, axis=-1)
    rot = jnp.concatenate([_rotate_half(t_row), _rotate_half(t_col)], axis=-1)
    cos = cos[None, :, None, None, :]
    sin = sin[None, :, None, None, :]
    return (t * cos + rot * sin).astype(t.dtype)


def diff_attend(q, k, v, lam):
    s = jnp.einsum('bqhcd,bkhcd->bhcqk', q, k).astype(jnp.float32) * (DH_A ** -0.5)
    p = jax.nn.softmax(s, axis=-1)
    a = p[:, :, 0] - lam * p[:, :, 1]
    return jnp.einsum('bhqk,bkhd->bqhd', a, v.astype(jnp.float32))


def diff_attend_blocks(q, k, v, lam):
    b, l = q.shape[:2]
    nb = l // Q_BLOCK
    qb = q.reshape(b, nb, Q_BLOCK, N_HEADS_A, 2, DH_A).transpose(1, 0, 2, 3, 4, 5)
    o = lax.map(lambda blk: diff_attend(blk, k, v, lam), qb)
    return o.transpose(1, 0, 2, 3, 4).reshape(b, l, N_HEADS_A, DV_A)


def diff_head_norm(o, w, lam_init):
    b, l = o.shape[:2]
    return (rms_norm(o, w) * (1.0 - lam_init)).reshape(b, l, WIDTH_A)


def log_forget(z, lb):
    zf = z.astype(jnp.float32)
    return jnp.logaddexp(jnp.log(lb), jnp.log1p(-lb) + jax.nn.log_sigmoid(zf))


def hgrn_scan(q, k, v, logf, s0):
    b, l, h, _ = q.shape
    dv = v.shape[-1]
    nc = l // CHUNK

    def chunks(a):
        return a.reshape(b, nc, CHUNK, h, a.shape[-1]).transpose(1, 0, 3, 2, 4)

    lower = jnp.tril(jnp.ones((CHUNK, CHUNK), dtype=bool))[:, :, None]

    def step(state, blk):
        qc, kc, vc, gc = blk
        cum = jnp.cumsum(gc, axis=2)
        rel = jnp.where(lower, cum[:, :, :, None, :] - cum[:, :, None, :, :], -jnp.inf)
        scores = jnp.einsum('bhtd,bhsd,bhtsd->bhts', qc, kc, jnp.exp(rel))
        o = (jnp.einsum('bhts,bhsv->bhtv', scores, vc)
             + jnp.einsum('bhtd,bhdv->bhtv', qc * jnp.exp(cum), state))
        cum_end = cum[:, :, -1:, :]
        state = (jnp.exp(cum_end)[:, :, 0, :, None] * state
                 + jnp.einsum('bhsd,bhsv->bhdv', kc * jnp.exp(cum_end - cum), vc))
        return state, o

    state, o = lax.scan(step, s0, (chunks(q), chunks(k), chunks(v), chunks(logf)))
    return o.transpose(1, 0, 3, 2, 4).reshape(b, l, h, dv), state


def hgrn_final_state(k, v, logf):
    tail = flip(jnp.cumsum(flip(logf), axis=1)) - logf
    return jnp.einsum('blhk,blhv->bhkv', k * jnp.exp(tail), v)


def hgrn_bidir(m, s_fwd, s_bwd):
    o_f, st_f = hgrn_scan(m['qh'], m['kf'], m['vh'], m['gf'], s_fwd)
    o_b, st_b = hgrn_scan(flip(m['qh']), flip(m['kb']), flip(m['vh']), flip(m['gb']), s_bwd)
    return o_f + flip(o_b), st_f, st_b


def hgrn_out(o, og, w):
    b, l = o.shape[:2]
    return (rms_norm(o, w) * jax.nn.silu(og.astype(jnp.float32))).reshape(b, l, WIDTH_B)


def fourier_mix(u, w):
    b, l, _ = u.shape
    ug = u.astype(jnp.float32).reshape(b, l, FN_GROUPS, FN_GROUP_DIM)
    y = jnp.fft.fft2(ug, axes=(1, 3), norm='ortho').real.reshape(b, l, WIDTH_C)
    return y.astype(u.dtype) @ w


def conv_ffn(h, w_up, conv_w, conv_b, w_down):
    u = h @ w_up
    up = jnp.pad(u, ((0, 0), (1, 1), (0, 0)))
    u = up[:, :-2] * conv_w[0] + up[:, 1:-1] * conv_w[1] + up[:, 2:] * conv_w[2] + conv_b
    gate, val = jnp.split(u, 2, axis=-1)
    return (jax.nn.silu(gate) * val) @ w_down


def unpack(p, lb):
    b, l, _ = p.shape
    idx = np.cumsum(PROJ_SIZES)[:-1].tolist()
    qa, ka, va, qh, zf, zb, ih, gh, uf = jnp.split(p, idx, axis=-1)
    gf = log_forget(zf, lb[0]).reshape(b, l, N_HEADS_B, DK_B)
    gb = log_forget(zb, lb[1]).reshape(b, l, N_HEADS_B, DK_B)
    return {
        'qa': qa.reshape(b, l, N_HEADS_A, 2, DH_A),
        'ka': ka.reshape(b, l, N_HEADS_A, 2, DH_A),
        'va': va.reshape(b, l, N_HEADS_A, DV_A),
        'qh': qh.reshape(b, l, N_HEADS_B, DK_B).astype(jnp.float32),
        'gf': gf, 'gb': gb,
        'kf': -jnp.expm1(gf), 'kb': -jnp.expm1(gb),
        'vh': ih.reshape(b, l, N_HEADS_B, DV_B).astype(jnp.float32),
        'og': gh.reshape(b, l, N_HEADS_B, DV_B),
        'uf': uf,
    }


def merge(att, rec, four, w_out, dt):
    return jnp.concatenate([att.astype(dt), rec.astype(dt), four.astype(dt)], axis=-1) @ w_out


def setup_inputs(seed: int = 0) -> dict:
    key = jax.random.key(seed)
    ks = jax.random.split(key, 20)
    f32 = jnp.float32

    def nrm(k, shape, s):
        return s * jax.random.normal(k, shape, f32)

    return {
        'x': nrm(ks[0], (BATCH, SEQ, D_MODEL), 1.0),
        'c': nrm(ks[1], (BATCH, D_MODEL), 1.0),
        'ctx': nrm(ks[2], (BATCH, CTX_LEN, D_MODEL), 1.0),
        'c_ctx': nrm(ks[3], (D_MODEL,), 1.0),
        'w_ada': nrm(ks[4], (DEPTH, D_MODEL, 6 * D_MODEL), 0.5 * D_MODEL ** -0.5),
        'b_ada': nrm(ks[5], (DEPTH, 6 * D_MODEL), 0.02),
        'norm1_w': 1.0 + nrm(ks[6], (DEPTH, D_MODEL), 0.05),
        'norm2_w': 1.0 + nrm(ks[7], (DEPTH, D_MODEL), 0.05),
        'w_in': nrm(ks[8], (DEPTH, D_MODEL, PROJ_WIDTH), D_MODEL ** -0.5),
        'lam_qk': nrm(ks[9], (DEPTH, 4, DH_A), 0.1),
        'subln_w': 1.0 + nrm(ks[10], (DEPTH, DV_A), 0.05),
        'lb_param': nrm(ks[11], (DEPTH, 2, WIDTH_BK), 0.5),
        'hgrn_norm_w': 1.0 + nrm(ks[12], (DEPTH, DV_B), 0.05),
        'w_fnet': nrm(ks[13], (DEPTH, WIDTH_C, WIDTH_C), WIDTH_C ** -0.5),
        'w_out': nrm(ks[14], (DEPTH, MIX_WIDTH, D_MODEL), MIX_WIDTH ** -0.5),
        'w_up': nrm(ks[15], (DEPTH, D_MODEL, 2 * D_FF), D_MODEL ** -0.5),
        'conv_w': jnp.array([0.0, 1.0, 0.0], f32)[None, :, None] + nrm(ks[16], (DEPTH, CONV_W, 2 * D_FF), 0.2),
        'conv_b': nrm(ks[17], (DEPTH, 2 * D_FF), 0.02),
        'w_down': nrm(ks[18], (DEPTH, D_FF, D_MODEL), D_FF ** -0.5),
        'final_norm_w': 1.0 + nrm(ks[19], (D_MODEL,), 0.05),
    }


def reference(x, c, ctx, c_ctx, w_ada, b_ada, norm1_w, norm2_w, w_in, lam_qk, subln_w,
              lb_param, hgrn_norm_w, w_fnet, w_out, w_up, conv_w, conv_b, w_down, final_norm_w):
    b, l, _ = x.shape
    rows = l // GRID_W
    cos, sin = axial_rope(rows)
    lb_all = jnp.cumsum(jax.nn.softmax(lb_param.astype(jnp.float32), axis=0), axis=0)
    lb_all = lb_all - lb_all[0:1]
    c_act = jax.nn.silu(c)
    cc_act = jax.nn.silu(c_ctx)
    xc = ctx
    for li in range(DEPTH):
        last = li == DEPTH - 1
        dt = x.dtype
        mod = c_act @ w_ada[li] + b_ada[li]
        sh1, sc1, g1, sh2, sc2, g2 = jnp.split(mod[:, None, :], 6, axis=-1)
        mod_c = cc_act @ w_ada[li] + b_ada[li]
        sh1c, sc1c, g1c, sh2c, sc2c, g2c = jnp.split(mod_c[None, None, :], 6, axis=-1)
        lam_init = 0.8 - 0.6 * math.exp(-0.3 * li)
        lq1, lk1, lq2, lk2 = lam_qk[li].astype(jnp.float32)
        lam = jnp.exp(jnp.sum(lq1 * lk1)) - jnp.exp(jnp.sum(lq2 * lk2)) + lam_init

        m = unpack(modulate(rms_norm(x, norm1_w[li]), sh1, sc1) @ w_in[li], lb_all[li])
        mc = unpack(modulate(rms_norm(xc, norm1_w[li]), sh1c, sc1c) @ w_in[li], lb_all[li])

        k_all = jnp.concatenate([mc['ka'], apply_axial_rope(m['ka'], cos, sin)], axis=1)
        v_all = jnp.concatenate([mc['va'], m['va']], axis=1)
        att = diff_head_norm(diff_attend_blocks(apply_axial_rope(m['qa'], cos, sin), k_all, v_all, lam),
                             subln_w[li], lam_init)

        if last:
            s_f = hgrn_final_state(mc['kf'], mc['vh'], mc['gf'])
            s_b = hgrn_final_state(flip(mc['kb']), flip(mc['vh']), flip(mc['gb']))
        else:
            zeros = jnp.zeros((xc.shape[0], N_HEADS_B, DK_B, DV_B), jnp.float32)
            o_c, s_f, s_b = hgrn_bidir(mc, zeros, zeros)
        o_l, _, _ = hgrn_bidir(m, s_f, s_b)
        rec = hgrn_out(o_l, m['og'], hgrn_norm_w[li])

        four = fourier_mix(m['uf'], w_fnet[li])

        x = x + g1 * merge(att, rec, four, w_out[li], dt)
        x = x + g2 * conv_ffn(modulate(rms_norm(x, norm2_w[li]), sh2, sc2),
                              w_up[li], conv_w[li], conv_b[li], w_down[li])

        if not last:
            att_c = diff_head_norm(diff_attend(mc['qa'], mc['ka'], mc['va'], lam), subln_w[li], lam_init)
            rec_c = hgrn_out(o_c, mc['og'], hgrn_norm_w[li])
            four_c = fourier_mix(mc['uf'], w_fnet[li])
            xc = xc + g1c * merge(att_c, rec_c, four_c, w_out[li], dt)
            xc = xc + g2c * conv_ffn(modulate(rms_norm(xc, norm2_w[li]), sh2c, sc2c),
                                     w_up[li], conv_w[li], conv_b[li], w_down[li])
    return rms_norm(x, final_norm_w)
```

```python
import math
from contextlib import ExitStack
import numpy as np
import concourse.bass as bass
import concourse.mybir as mybir
from concourse.bass_utils import run_bass_kernel_spmd

F32 = mybir.dt.float32
BF16 = mybir.dt.bfloat16
AF = mybir.ActivationFunctionType
ALU = mybir.AluOpType
AX = mybir.AxisListType

ENGS = ("pe", "act", "dve", "pool", "sp")
EPOCH = 24000
NDSEM = 8
SB_BASE = 17408
SB_LIMIT = 229376

D = 1024
L = 2048
CL = 256
TT = CL + L
NT = TT // 128
NCH = TT // 64
DFF = 2816
NPAIR = 22
EPS = 1e-6


class Sched:
    def __init__(self):
        self.ops = {e: [] for e in ENGS}
        self.cnt = {e: 0 for e in ENGS}
        self.last_w = {}
        self.readers = {}
        self.waited = {e: {} for e in ENGS}
        self.ndma = {"sp": 0, "pool": 0}
        self.all_tokens = {}
        self.semkeys = set()

    def _need(self, eng, tok, waits):
        sk, val = tok
        if sk[0] == "pe" and eng == "pe":
            return
        if self.waited[eng].get(sk, 0) >= val:
            return
        self.waited[eng][sk] = val
        waits.append((sk, val))

    def _deps(self, eng, reads, writes):
        waits = []
        for k in reads:
            w = self.last_w.get(k)
            if w is not None:
                self._need(eng, w, waits)
        for k in writes:
            w = self.last_w.get(k)
            if w is not None:
                self._need(eng, w, waits)
            for r in self.readers.get(k, ()):
                self._need(eng, r, waits)
        return waits

    def _commit(self, tok, reads, writes):
        for k in reads:
            self.readers.setdefault(k, []).append(tok)
        for k in writes:
            self.last_w[k] = tok
            self.readers[k] = []
        sk, val = tok
        self.all_tokens[sk] = max(self.all_tokens.get(sk, 0), val)
        self.semkeys.add(sk)

    def op(self, eng, fn, reads=(), writes=()):
        waits = self._deps(eng, reads, writes)
        self.cnt[eng] += 1
        c = self.cnt[eng]
        sk = (eng, (c - 1) // EPOCH)
        tok = (sk, (c - 1) % EPOCH + 1)
        self.ops[eng].append((waits, fn, (sk, 1)))
        self._commit(tok, reads, writes)
        return tok

    def dma(self, q, fn, reads=(), writes=()):
        waits = self._deps(q, reads, writes)
        j = self.ndma[q]
        self.ndma[q] += 1
        sk = ("dma" + q, j % NDSEM)
        val = 16 * (j // NDSEM + 1)
        if val > 16:
            self._need(q, (sk, val - 16), waits)
        tok = (sk, val)
        self.ops[q].append((waits, fn, (sk, 16)))
        self._commit(tok, reads, writes)
        return tok

    def barrier(self):
        for e in ENGS:
            waits = []
            for sk, val in self.all_tokens.items():
                self._need(e, (sk, val), waits)
            if waits:
                self.ops[e].append((waits, None, None))
        self.last_w = {}
        self.readers = {}

    def final_wait(self, eng="sp"):
        waits = []
        for sk, val in self.all_tokens.items():
            self._need(eng, (sk, val), waits)
        if waits:
            self.ops[eng].append((waits, None, None))

    def emit(self, nc, stack):
        sems = {}
        for sk in sorted(self.semkeys, key=str):
            sems[sk] = stack.enter_context(nc.semaphore("s_" + "_".join(str(x) for x in sk)))
        block = stack.enter_context(nc.Block())

        def run(engname, e):
            for waits, fn, inc in self.ops[engname]:
                for sk, val in waits:
                    e.wait_ge(sems[sk], val)
                if fn is not None:
                    fn(e).then_inc(sems[inc[0]], inc[1])

        @block.tensor
        def _(e):
            run("pe", e)

        @block.scalar
        def _(e):
            run("act", e)

        @block.vector
        def _(e):
            run("dve", e)

        @block.gpsimd
        def _(e):
            run("pool", e)

        @block.sync
        def _(e):
            run("sp", e)


def _dsize(dt):
    return 2 if dt == BF16 else 4


class Builder:
    def __init__(self, dbg=False, nlayers=2, stop=None):
        self.dbg = dbg
        self.nlayers = nlayers
        self.stop = stop
        self.nc = bass.Bass("TRN2", target_bir_lowering=False)
        self.S = Sched()
        self.off = SB_BASE
        self.nalloc = 0
        self.bank_rr = 0
        self.stg_rr = 0
        self.stg_limit = 6
        self.din = {}
        self.offs = {}

    def alloc(self, shape, dt, name="t"):
        nb = int(np.prod(shape[1:])) * _dsize(dt)
        nb = (nb + 31) // 32 * 32
        self.nalloc += 1
        h = self.nc.alloc_sbuf_tensor_at(f"{name}_{self.nalloc}", list(shape), dt, offset=self.off)
        self.offs[name] = self.off
        self.off += nb
        assert self.off <= SB_LIMIT, (name, self.off)
        return h

    def inp(self, name, shape):
        t = self.nc.dram_tensor(name, list(shape), F32, kind="ExternalInput").ap()
        self.din[name] = t
        return t

    def scratch(self, name, shape, dt):
        kind = "ExternalOutput" if self.dbg else "Internal"
        return self.nc.dram_tensor(name, list(shape), dt, kind=kind).ap()

    def bank(self, lo=0, hi=8):
        b = lo + self.bank_rr % (hi - lo)
        self.bank_rr += 1
        return b

    def pb(self, b, w=512, p=128):
        return self.ps[0:p, b * 512:b * 512 + w]

    def mm(self, out, lhsT, rhs, start, stop, reads, writes):
        self.S.op("pe", lambda e: e.matmul(out, lhsT=lhsT, rhs=rhs, start=start, stop=stop), reads, writes)

    def tr(self, out, in_, reads, writes):
        idn = self.IDN
        self.S.op("pe", lambda e: e.transpose(out=out, in_=in_, identity=idn[:, :]), list(reads) + ["const"], writes)

    def act(self, out, in_, func, reads, writes, scale=None, bias=None):
        kw = {}
        if scale is not None:
            kw["scale"] = scale
        if bias is not None:
            kw["bias"] = bias
        self.S.op("act", lambda e: e.activation(out=out, in_=in_, func=func, **kw), reads, writes)

    def tt(self, out, in0, in1, op, reads, writes, eng="dve"):
        self.S.op(eng, lambda e: e.tensor_tensor(out=out, in0=in0, in1=in1, op=op), reads, writes)

    def ts(self, out, in0, s1, s2, op0, op1, reads, writes, eng="dve"):
        if op1 is None:
            self.S.op(eng, lambda e: e.tensor_scalar(out=out, in0=in0, scalar1=s1, scalar2=None, op0=op0), reads, writes)
        else:
            self.S.op(eng, lambda e: e.tensor_scalar(out=out, in0=in0, scalar1=s1, scalar2=s2, op0=op0, op1=op1),
                      reads, writes)

    def stt(self, out, in0, scalar, in1, op0, op1, reads, writes):
        self.S.op("dve", lambda e: e.scalar_tensor_tensor(out=out, in0=in0, scalar=scalar, in1=in1, op0=op0, op1=op1),
                  reads, writes)

    def copy(self, out, in_, reads, writes, eng="dve"):
        self.S.op(eng, lambda e: e.tensor_copy(out=out, in_=in_), reads, writes)

    def recip(self, out, in_, reads, writes):
        self.S.op("dve", lambda e: e.reciprocal(out=out, in_=in_), reads, writes)

    def memset(self, ap, v, writes, eng="dve"):
        self.S.op(eng, lambda e: e.memset(ap, v), (), writes)

    def dma(self, out, in_, reads, writes, q="sp"):
        self.S.dma(q, lambda e: e.dma_start(out=out, in_=in_), reads, writes)

    def stage(self, dt, w=512, p=128):
        i = self.stg_rr % self.stg_limit
        self.stg_rr += 1
        t = self.STG[i]
        if dt == F32:
            ap = t[0:p, 0:w]
        else:
            ap = t[0:p, 0:w // 2].bitcast(BF16)
        return ap, ("stg", i)

    def build(self):
        nc = self.nc
        with ExitStack() as st:
            self.ps = st.enter_context(nc.psum_tensor("ps", [128, 4096], F32))
            self.declare_io()
            self.setup_consts()
            for li in range(self.nlayers):
                self.layer(li)
                if self.stop is not None and self.stop[0] == li:
                    break
            else:
                if not getattr(self, "final_fused", False):
                    self.final_norm()
            if self.dbg:
                self.dump_x()
            self.S.final_wait("sp")
            self.S.emit(nc, st)
        return nc

    def declare_io(self):
        self.xT_d = self.inp("xT", [128, 8, L])
        self.cT_d = self.inp("ctxT", [128, 8, CL])
        self.cvec_d = self.inp("cvec", [128, 8, 2])
        self.wada_d = self.inp("w_ada", [2, 128, 8, 6 * D])
        self.bada_d = self.inp("b_ada", [128, 2, 48])
        self.n1_d = self.inp("norm1_w", [128, 2, 8])
        self.n2_d = self.inp("norm2_w", [128, 2, 8])
        self.fn_d = self.inp("final_norm_w", [128, 8])
        self.win_d = self.inp("w_in", [2, 128, 8, 3072])
        self.lam_d = self.inp("lam_qk", [128, 2, 256])
        self.subw_d = self.inp("subln_w", [128, 2])
        self.lb_d = self.inp("lb_param", [128, 2, 2, 2])
        self.hnw_d = self.inp("hgrn_norm_w", [128, 2])
        self.wf_d = self.inp("w_fnet", [2, 128, 2, 256])
        self.wo_d = self.inp("w_out", [2, 128, 8, D])
        self.wu_d = self.inp("w_up", [2, 128, 8, 2 * DFF])
        self.cw_d = self.inp("conv_w", [128, 2, 44, 3])
        self.cb_d = self.inp("conv_b", [128, 2, 44])
        self.wd_d = self.inp("w_down", [2, 128, NPAIR, D])
        self.k_rope_d = self.inp("k_rope", [128, 2, L])
        self.k_mats_d = self.inp("k_mats", [128, 7, 128])
        self.k_scan_d = self.inp("k_scan", [128, TT])
        self.k_dft64_d = self.inp("k_dft64", [128, 2, 2, 256])
        self.k_dftL_d = self.inp("k_dftL", [2, 128, 16, L])
        self.k_dftC_d = self.inp("k_dftC", [128, 2, 2, CL])
        self.outT_d = self.nc.dram_tensor("outT", [128, 8, L], F32, kind="ExternalOutput").ap()
        self.QK = self.scratch("s_qk", [1024, TT], BF16)
        self.GZ = self.scratch("s_gz", [1024, TT], F32)
        self.UF = self.scratch("s_uf", [256, TT], BF16)
        self.VT = self.scratch("s_vt", [TT, 768], BF16)
        self.MIX = self.scratch("s_mix", [1024, TT], BF16)
        self.V12D = self.scratch("s_v12", [TT, 512], BF16)
        if self.dbg:
            self.XD = self.nc.dram_tensor("s_x", [128, 8, TT], F32, kind="ExternalOutput").ap()
            self.MODD = self.nc.dram_tensor("s_mod", [128, 2, 48, 2], F32, kind="ExternalOutput").ap()

    def setup_consts(self):
        A = self.alloc
        self.X = A([128, 8, TT], F32, "X")
        self.ROPE = A([128, 2, L], BF16, "rope")
        self.MATS = A([128, 7, 128], BF16, "mats")
        self.SCANM = A([128, TT], BF16, "scanm")
        self.DFT64 = A([128, 2, 2, 256], BF16, "dft64")
        self.DFTC = A([128, 2, 2, CL], BF16, "dftc")
        self.CV = A([128, 8, 2], F32, "cv")
        self.CACT = A([128, 8, 2], BF16, "cact")
        self.BADA = A([128, 2, 48], F32, "bada")
        self.MOD = A([128, 2, 48, 2], F32, "mod")
        self.N1 = A([128, 2, 8], F32, "n1")
        self.N2 = A([128, 2, 8], F32, "n2")
        self.FNW = A([128, 8], F32, "fnw")
        self.AB = A([128, 2, 2, 8, 2], F32, "ab")
        self.LAMQ = A([128, 2, 256], F32, "lamq")
        self.SUBW = A([128, 2], F32, "subw")
        self.LBP = A([128, 2, 2, 2], F32, "lbp")
        self.LB = A([128, 2, 2, 2], F32, "lb")
        self.HNW = A([128, 2], F32, "hnw")
        self.CW = A([128, 2, 44, 3], F32, "cw")
        self.CB = A([128, 2, 44], F32, "cb")
        self.EPS_T = A([128, 1], F32, "eps")
        self.ZERO_T = A([128, 1], F32, "zero")
        self.ONE_T = A([128, 1], F32, "one")
        self.SM = A([128, 16], F32, "small")
        self.STG = [A([128, 512], F32, f"stg{i}") for i in range(6)]
        self.WA = [A([128, 8, 128], BF16, f"wap{i}") for i in range(2)]
        self.mod_next = {0: 0, 1: 0}
        self.IDN = self.MATS[:, 0, :]
        self.RPM = self.MATS[:, 1, :]
        self.ONES = self.MATS[:, 2, :]
        self.MASKF = self.MATS[:, 3, :]
        self.MASKB = self.MATS[:, 4, :]
        self.HA = self.MATS[:, 5, :]
        self.HB = self.MATS[:, 6, :]
        d = self.dma
        d(self.X[:, :, 0:CL], self.cT_d, [], ["X"])
        d(self.X[:, :, CL:TT], self.xT_d, [], ["X"])
        d(self.MATS[:, :, :], self.k_mats_d, [], ["const"], q="pool")
        d(self.SCANM[:, :], self.k_scan_d, [], ["const"], q="pool")
        d(self.DFT64[:, :, :, :], self.k_dft64_d, [], ["const"], q="pool")
        d(self.DFTC[:, :, :, :], self.k_dftC_d, [], ["const"], q="pool")
        d(self.CV[:, :, :], self.cvec_d, [], ["const"])
        d(self.BADA[:, :, :], self.bada_d, [], ["const"])
        d(self.N1[:, :, :], self.n1_d, [], ["const"])
        d(self.N2[:, :, :], self.n2_d, [], ["const"])
        d(self.FNW[:, :], self.fn_d, [], ["const"])
        d(self.LAMQ[:, :, :], self.lam_d, [], ["const"])
        d(self.SUBW[:, :], self.subw_d, [], ["const"])
        d(self.LBP[:, :, :, :], self.lb_d, [], ["const"])
        d(self.HNW[:, :], self.hnw_d, [], ["const"])
        d(self.CW[:, :, :, :], self.cw_d, [], ["const"])
        d(self.CB[:, :, :], self.cb_d, [], ["const"])
        self.memset(self.EPS_T[:, :], EPS, ["const"])
        self.memset(self.ZERO_T[:, :], 0.0, ["const"])
        self.memset(self.ONE_T[:, :], 1.0, ["const"])
        self.S.barrier()
        self.act(self.CACT[:, :, :], self.CV[:, :, :], AF.Silu, ["const"], ["cact"])
        lbe = self.STG[0][:, 0:8].rearrange("p (a b c) -> p a b c", a=2, b=2)
        self.act(lbe, self.LBP[:, :, :, :], AF.Exp, ["const"], [("stg", 0)])
        den = self.STG[1][:, 0:4].rearrange("p (b c) -> p b c", b=2)
        self.tt(den, lbe[:, 0, :, :], lbe[:, 1, :, :], ALU.add, [("stg", 0)], [("stg", 1)])
        self.recip(den, den, [], [("stg", 1)])
        self.memset(self.LB[:, 0, :, :], 0.0, ["lb"])
        self.tt(self.LB[:, 1, :, :], lbe[:, 1, :, :], den, ALU.mult, [("stg", 0), ("stg", 1)], ["lb"])
        self.S.barrier()
        self.persist_mark = self.off

    def layer(self, li):
        self.li = li
        self.last = li == 1
        self.off = self.persist_mark
        self.phase_mod(li)
        if self.stop == (li, "mod"):
            return
        self.off = self.persist_mark
        self.phase_proj(li)
        self.S.barrier()
        if self.stop == (li, "proj"):
            return
        self.off = self.persist_mark
        self.phase_fnet_a(li)
        self.S.barrier()
        self.off = self.persist_mark
        self.phase_hgrn(li)
        self.S.barrier()
        if self.stop in ((li, "hgrn"), (li, "fnet")):
            return
        self.off = self.persist_mark
        self.phase_att(li)
        self.S.barrier()
        if self.stop == (li, "att"):
            return
        self.off = self.persist_mark
        self.phase_wout(li)
        self.S.barrier()
        if self.stop == (li, "wout"):
            return
        self.off = self.persist_mark
        self.phase_ffn(li)
        self.S.barrier()

    def mod_pump(self, li, npieces, bank_fn=None):
        for _ in range(npieces):
            pc = self.mod_next[li]
            if pc >= 48:
                return
            self.mod_next[li] += 1
            wa = self.WA[pc % 2]
            key = ("wa", pc % 2)
            self.dma(wa[:, :, :], self.wada_d[li, :, :, pc * 128:(pc + 1) * 128], [], [key], q="pool")
            b = bank_fn() if bank_fn is not None else self.bank()
            for k in range(8):
                self.mm(self.pb(b, 2), wa[:, k, :], self.CACT[:, k, :], k == 0, k == 7, [key, "cact"], [("ps", b)])
            self.ts(self.MOD[:, li, pc, :], self.pb(b, 2), self.BADA[:, li, pc:pc + 1], None, ALU.add, None, ["const"],
                    [("ps", b), ("modp", li, pc // 8)])

    def mod_ab(self, li, wi):
        NW, jsh, jsc = ((self.N1, 0, 8), (self.N2, 24, 32))[wi]
        for n in range(2):
            self.stt(self.AB[:, wi, 0, :, n], self.MOD[:, li, jsc:jsc + 8, n], 1.0, NW[:, li, :], ALU.add, ALU.mult,
                     [("modp", li, jsc // 8), "const"], ["ab"])
            self.copy(self.AB[:, wi, 1, :, n], self.MOD[:, li, jsh:jsh + 8, n], [("modp", li, jsh // 8)], ["ab"])

    def phase_mod(self, li):
        if li == 0:
            self.mod_pump(0, 16)
        else:
            self.mod_pump(li, 48)
            self.mod_ab(li, 1)
        self.mod_ab(li, 0)
        lam_init = 0.8 - 0.6 * math.exp(-0.3 * li)
        self.lam_init = lam_init
        t0 = self.STG[2][:, 0:128]
        s12 = self.SM[:, 0:2]
        for i in range(2):
            self.tt(t0[:, i * 64:(i + 1) * 64], self.LAMQ[:, li, i * 128:i * 128 + 64],
                    self.LAMQ[:, li, i * 128 + 64:i * 128 + 128], ALU.mult, ["const"], [("stg", 2)])
        self.S.op("dve", lambda e: e.reduce_sum(out=s12, in_=t0.rearrange("p (i d) -> p i d", i=2), axis=AX.X),
                  [("stg", 2)], ["sm"])
        self.act(s12, s12, AF.Exp, [], ["sm"])
        self.stt(self.SM[:, 2:3], self.SM[:, 1:2], -lam_init, self.SM[:, 0:1], ALU.add, ALU.subtract, [], ["sm"])
        self.ts(self.SM[:, 3:4], self.SUBW[:, li:li + 1], 1.0 - lam_init, None, ALU.mult, None, ["const"], ["sm"])
        oml = self.SM[:, 4:8].rearrange("p (a b) -> p a b", a=2)
        self.ts(oml, self.LB[:, li, :, :], -1.0, 1.0, ALU.mult, ALU.add, ["lb"], ["sm"])

    def rstd_bcast(self, xv, w, RS, rskey, xkeys, sqbuf, sqkey):
        self.act(sqbuf[:, :, 0:w], xv, AF.Square, xkeys, [sqkey])
        b = self.bank()
        for k in range(8):
            self.mm(self.pb(b, w), self.ONES, sqbuf[:, k, 0:w], k == 0, k == 7, [sqkey, "const"], [("ps", b)])
        self.act(RS[:, 0:w], self.pb(b, w), AF.Identity, ["const"], [("ps", b), rskey], scale=1.0 / D,
                 bias=self.EPS_T[:, 0:1])
        self.act(RS[:, 0:w], RS[:, 0:w], AF.Ln, [], [rskey])
        self.act(RS[:, 0:w], RS[:, 0:w], AF.Exp, [], [rskey], scale=-0.5)

    def norm_mod(self, tok0, w, n, wi, HT, htkey, tmp, xkey="X"):
        SQ, RS, T1 = tmp
        xv = self.X[:, :, tok0:tok0 + w]
        self.rstd_bcast(xv, w, RS, "rs", [xkey], SQ, "sq")
        for k in range(8):
            t1 = T1[k % 2]
            self.tt(t1[:, 0:w], self.X[:, k, tok0:tok0 + w], RS[:, 0:w], ALU.mult, [xkey, "rs"], [("t1", k % 2)])
            self.act(HT[:, k, 0:w], t1[:, 0:w], AF.Identity, [("t1", k % 2), "ab"], [htkey],
                     scale=self.AB[:, wi, 0, k, n:n + 1], bias=self.AB[:, wi, 1, k, n:n + 1])

    def groups(self, ctx=True):
        g = [(0, CL, 1)] if ctx else []
        for i in range(4):
            g.append((CL + 512 * i, 512, 0))
        return g

    def phase_proj(self, li):
        WIN = self.alloc([128, 8, 3072], BF16, "win")
        self.dma(self.ROPE[:, :, :], self.k_rope_d, [], ["rope"], q="pool")
        for pc in range(12):
            self.dma(WIN[:, :, pc * 256:(pc + 1) * 256], self.win_d[li, :, :, pc * 256:(pc + 1) * 256], [], [("win", pc)],
                     q="pool")
        HTs = [self.alloc([128, 8, 512], BF16, f"ht{i}") for i in range(2)]
        SQ = self.alloc([128, 8, 512], BF16, "sq")
        RS = self.alloc([128, 512], F32, "rs")
        T1 = [self.alloc([128, 512], F32, f"t1{i}") for i in range(2)]
        QS = [self.alloc([128, 512], BF16, f"qs{i}") for i in range(2)]
        TA = [self.alloc([128, 512], F32, f"ta{i}") for i in range(2)]
        TB = [self.alloc([128, 512], F32, f"tb{i}") for i in range(2)]
        rr = 0
        grps = self.groups()
        t0_, w0_, n0_ = grps[0]
        xk = (lambda n_: "X")
        self.norm_mod(t0_, w0_, n0_, 0, HTs[0], ("ht", 0), (SQ, RS, T1), xkey=xk(n0_))
        for gi, (tok0, w, n) in enumerate(grps):
            HT = HTs[gi % 2]
            hk = ("ht", gi % 2)
            if gi + 1 < len(grps):
                t1_, w1_, n1_ = grps[gi + 1]
                self.norm_mod(t1_, w1_, n1_, 0, HTs[(gi + 1) % 2], ("ht", (gi + 1) % 2), (SQ, RS, T1), xkey=xk(n1_))
            rope_pending = []
            for cc in list(range(0, 8)) + [12, 13, 14, 15, 16, 17, 20, 21, 22, 23]:
                b = self.bank()
                for k in range(8):
                    self.mm(self.pb(b, w), WIN[:, k, cc * 128:(cc + 1) * 128], HT[:, k, 0:w], k == 0, k == 7,
                            [("win", cc // 2), hk], [("ps", b)])
                while len(rope_pending) > (1 if cc < 8 else 0):
                    rope_pending.pop(0)()
                if cc < 8:
                    dst = self.QK[cc * 128:(cc + 1) * 128, tok0:tok0 + w]
                    if n == 1:
                        sg, sk = self.stage(BF16, w)
                        self.act(sg, self.pb(b, w), AF.Copy, [], [("ps", b), sk])
                        self.dma(dst, sg, [sk], ["QK"])
                    else:
                        i = rr % 2
                        rr += 1
                        lt0 = tok0 - CL
                        self.act(QS[i][:, 0:w], self.pb(b, w), AF.Copy, [], [("ps", b), ("qs", i)])
                        self.tt(TA[i][:, 0:w], self.pb(b, w), self.ROPE[:, 0, lt0:lt0 + w], ALU.mult, ["rope"],
                                [("ps", b), ("ta", i)])

                        def rope2(i=i, w=w, lt0=lt0, dst=dst):
                            b2 = self.bank()
                            self.mm(self.pb(b2, w), self.RPM, QS[i][:, 0:w], True, True, [("qs", i), "const"], [("ps", b2)])
                            self.tt(TB[i][:, 0:w], self.pb(b2, w), self.ROPE[:, 1, lt0:lt0 + w], ALU.mult, ["rope"],
                                    [("ps", b2), ("tb", i)])
                            sg, sk = self.stage(BF16, w)
                            self.tt(sg, TA[i][:, 0:w], TB[i][:, 0:w], ALU.add, [("ta", i), ("tb", i)], [sk])
                            self.dma(dst, sg, [sk], ["QK"])

                        rope_pending.append(rope2)
                elif cc >= 22:
                    sg, sk = self.stage(BF16, w)
                    self.act(sg, self.pb(b, w), AF.Copy, [], [("ps", b), sk])
                    self.dma(self.UF[(cc - 22) * 128:(cc - 21) * 128, tok0:tok0 + w], sg, [sk], ["UF"])
                else:
                    row = {12: 0, 13: 128, 14: 256, 15: 384, 16: 512, 17: 640, 20: 768, 21: 896}[cc]
                    sg, sk = self.stage(F32, w)
                    self.act(sg, self.pb(b, w), AF.Copy, [], [("ps", b), sk])
                    self.dma(self.GZ[row:row + 128, tok0:tok0 + w], sg, [sk], ["GZ"])
                if li == 0 and cc % 2 == 1:
                    self.mod_pump(0, 1)
            for t in range(w // 128):
                tg = tok0 + t * 128
                for (c0, cw, vc0) in ((1024, 512, 0), (2304, 256, 512)):
                    b = self.bank()
                    for k in range(8):
                        self.mm(self.pb(b, cw), HT[:, k, t * 128:(t + 1) * 128], WIN[:, k, c0:c0 + cw], k == 0, k == 7,
                                [("win", c0 // 256), ("win", (c0 + cw - 1) // 256), hk], [("ps", b)])
                    sg, sk = self.stage(BF16, cw)
                    self.copy(sg, self.pb(b, cw), [], [("ps", b), sk])
                    self.dma(self.VT[tg:tg + 128, vc0:vc0 + cw], sg, [sk], ["VT"])
        if li == 0:
            self.mod_pump(0, 48)
            self.mod_ab(0, 1)

    def phase_hgrn(self, li):
        A = self.alloc
        G = [A([128, TT], F32, f"g{i}") for i in range(3)]
        QH = A([128, TT], F32, "qh")
        QTL = [A([128, TT], BF16, f"qtl{i}") for i in range(2)]
        KTL = [A([128, TT], BF16, f"ktl{i}") for i in range(2)]
        KTM = [A([128, NT, 128], BF16, f"ktm{i}") for i in range(2)]
        VTM = A([128, NT, 128], BF16, "vtm")
        ST = [A([128, NCH, 128], BF16, f"st{i}") for i in range(2)]
        SS = [A([128, 128], F32, f"ss{i}") for i in range(2)]
        TMPS = [[A([128, 128], F32, f"tmps{d}{i}") for i in range(2)] for d in range(2)]
        TMPS2 = [A([128, 128], F32, f"ssb{d}") for d in range(2)]
        EE = [A([128, 5, NCH], F32, f"ee{i}") for i in range(2)]
        save = self.off
        self.off = self.offs["rope"]
        KX = [A([128, 3, 128], BF16, f"kx{i}") for i in range(6)]
        AM = [A([128, 2, 128], BF16, f"am{i}") for i in range(4)]
        SQ = A([64, 512], BF16, "osq")
        assert self.off <= self.offs["rope"] + 8192
        self.off = save
        V12T = [A([128, 512], BF16, f"v12t{i}") for i in range(2)]
        SQs = [SQ, A([64, 512], BF16, "osq1")]
        OGs = [G[1], QH]
        MASK2 = self.MATS[:, 3:5, :]
        self.stg_limit = 2
        t_lo = 2 if self.last else 0
        for hp in range(2):
            self.fnet_begin(hp, V12T)
            first_loads = [True]
            for d in range(2):
                g0, g1, g2 = G
                ee = EE[d]
                HW, HC = TT // 2, NCH // 2
                H = range(2)

                def hs(buf, hh):
                    return buf[:, hh * HW:(hh + 1) * HW]

                def hv(buf, hh):
                    return buf[:, hh * HW:(hh + 1) * HW].rearrange("p (c i) -> p c i", i=64)

                def es(k, hh):
                    return ee[:, k, hh * HC:(hh + 1) * HC]

                def eb(k, hh):
                    return ee[:, k, hh * HC:(hh + 1) * HC].unsqueeze(2).to_broadcast([128, HC, 64])

                r_lo = 256 + 256 * d + hp * 128
                for hh in H:
                    self.dma(hs(g0, hh), self.GZ[r_lo:r_lo + 128, hh * HW:(hh + 1) * HW], ["GZ"], [("g0", hh)])
                if first_loads[0]:
                    first_loads[0] = False
                    self.dma(QH[:, :], self.GZ[hp * 128:(hp + 1) * 128, :], ["GZ"], ["qh"])
                    self.dma(VTM[:, :, :],
                             self.VT[:, 512 + hp * 128:512 + (hp + 1) * 128].rearrange("(t p) c -> p t c", p=128),
                             ["VT"], ["vtm"])
                for hh in H:
                    self.act(hs(g0, hh), hs(g0, hh), AF.Sigmoid, [], [("g0", hh)])
                for hh in H:
                    self.act(hs(g0, hh), hs(g0, hh), AF.Identity, ["sm", "lb"], [("g0", hh)],
                             scale=self.SM[:, 4 + 2 * d + hp:5 + 2 * d + hp], bias=self.LB[:, li, d, hp:hp + 1])
                for hh in H:
                    self.act(hs(g1, hh), hs(g0, hh), AF.Ln, [("g0", hh)], [("g1", hh)])
                self.fnet_step()
                for hh in H:
                    self.S.op("dve", lambda e, o=hs(g2, hh), a=hs(self.SCANM, hh), bb=hs(g1, hh): e.tensor_tensor_scan(
                        out=o, data0=a, data1=bb, initial=0.0, op0=ALU.mult, op1=ALU.add), [("g1", hh), "const"],
                        [("g2", hh)])
                    self.copy(es(0, hh), hv(g2, hh)[:, :, 63], [("g2", hh)], [("ee", d, hh)])
                    if d == 1:
                        self.tt(hv(g2, hh), eb(0, hh), hv(g2, hh), ALU.subtract, [("ee", d, hh)], [("g2", hh)])
                        self.tt(hs(g2, hh), hs(g2, hh), hs(g1, hh), ALU.add, [("g1", hh)], [("g2", hh)])
                    mid = 31 if d == 0 else 32
                    self.copy(es(1, hh), hv(g2, hh)[:, :, mid], [("g2", hh)], [("ee", d, hh)])
                    self.tt(hv(g2, hh), hv(g2, hh), eb(1, hh), ALU.subtract, [("ee", d, hh)], [("g2", hh)])
                self.fnet_step()
                for hh in H:
                    self.act(hs(g1, hh), hs(g2, hh), AF.Exp, [("g2", hh)], [("g1", hh)], scale=-1.0)
                    self.act(hs(g2, hh), hs(g2, hh), AF.Exp, [], [("g2", hh)])
                self.fnet_step()
                for hh in H:
                    self.tt(hs(QTL[d], hh), hs(QH, hh), hs(g2, hh), ALU.mult, ["qh", ("g2", hh)], [("qtl", d, hh)])
                    self.act(hs(g0, hh), hs(g0, hh), AF.Identity, ["const"], [("g0", hh)], scale=-1.0,
                             bias=self.ONE_T[:, 0:1])
                    self.tt(hs(KTL[d], hh), hs(g0, hh), hs(g1, hh), ALU.mult, [("g0", hh), ("g1", hh)], [("ktl", d, hh)])
                    self.act(es(2, hh), es(0, hh), AF.Exp, [], [("ee", d, hh)])
                    self.tt(es(3, hh), es(0, hh), es(1, hh), ALU.subtract, [], [("ee", d, hh)])
                    self.act(es(3, hh), es(3, hh), AF.Exp, [], [("ee", d, hh)])
                    self.act(es(4, hh), es(1, hh), AF.Exp, [], [("ee", d, hh)])
                self.fnet_step()
                K2 = g1[:, 0:TT // 2].bitcast(BF16)
                for hh in H:
                    self.tt(hv(K2, hh), hv(KTL[d], hh), eb(3, hh), ALU.mult,
                            [("ktl", d, 0), ("ktl", d, 1), ("ee", d, hh)], [("g1", 0), ("k2", hh)])
                for t4 in range(0, NT, 4):
                    b = self.bank(0, 4)
                    nn = min(4, NT - t4)
                    pbb = self.pb(b).bitcast(BF16)
                    for i in range(nn):
                        t = t4 + i
                        self.tr(pbb[:, i * 128:(i + 1) * 128], K2[:, t * 128:(t + 1) * 128], [("g1", 0), ("k2", t // 9)],
                                [("ps", b)])
                    self.copy(KTM[d][:, t4:t4 + nn, :], pbb[:, 0:nn * 128].rearrange("p (t c) -> p t c", c=128), [],
                              [("ps", b), ("ktm", d)])
            orders = [list(range(NCH)), [3, 2, 1, 0] + list(range(NCH - 1, 3, -1))]
            SS2 = [[SS[d], TMPS2[d]] for d in range(2)]
            for d in range(2):
                self.memset(SS2[d][0][:, :], 0.0, [("ss", d, 0)])
            for n8 in range(0, NCH, 8):
                self.fnet_step()
                kvb = [{}, {}]
                for d in range(2):
                    chunk = orders[d][n8:n8 + 8]
                    bAB = (self.bank(0, 4), self.bank(0, 4))
                    cntb = [0, 0]
                    for n in chunk:
                        t, j = n // 2, n % 2
                        b = bAB[j]
                        i = cntb[j]
                        cntb[j] += 1
                        self.mm(self.pb(b)[:, i * 128:(i + 1) * 128], KTM[d][64 * j:64 * j + 64, t, :],
                                VTM[64 * j:64 * j + 64, t, :], True, True, [("ktm", d), "vtm"], [("ps", b)])
                        kvb[d][n] = (b, i)
                for ci in range(len(orders[0][n8:n8 + 8])):
                    for d in range(2):
                        n = orders[d][n8 + ci]
                        ee = EE[d]
                        b, i = kvb[d][n]
                        step = n8 + ci
                        s_cur, s_nxt = SS2[d][step % 2], SS2[d][(step + 1) % 2]
                        k_cur, k_nxt = ("ss", d, step % 2), ("ss", d, (step + 1) % 2)
                        self.act(ST[d][:, n, :], s_cur[:, :], AF.Identity, [k_cur, ("ee", d, 0), ("ee", d, 1)], [("st", d)],
                                 scale=ee[:, 4, n:n + 1], bias=self.ZERO_T[:, 0:1])
                        self.stt(s_nxt[:, :], s_cur[:, :], ee[:, 2, n:n + 1], self.pb(b)[:, i * 128:(i + 1) * 128], ALU.mult,
                                 ALU.add, [k_cur, ("ee", d, 0), ("ee", d, 1)], [("ps", b), k_nxt])
            self.fnet_finish()
            self.S.barrier()
            for hl in range(2):
                h = 2 * hp + hl
                og = OGs[hl]
                self.dma(og[0:64, :], self.GZ[768 + 64 * h:768 + 64 * h + 64, :], ["GZ"], [("og", hl)])
                self.act(og[0:64, :], og[0:64, :], AF.Silu, [], [("og", hl)])
            tiles = list(range(t_lo, NT))
            st = {"kx": 0, "am": 0, "ab": 0}

            def stage_kx(t):
                kxs = []
                for d in range(2):
                    kx = KX[st["kx"] % 6]
                    kk_ = ("kx", st["kx"] % 6)
                    st["kx"] += 1
                    H1 = self.HA if d == 0 else self.HB
                    H2 = self.HB if d == 0 else self.HA
                    kfull = KTL[d][:, t * 128:(t + 1) * 128]
                    qfull = QTL[d][:, t * 128:(t + 1) * 128]
                    self.tt(kx[:, 0, :], kfull, H1, ALU.mult, ["const"], [kk_])
                    self.tt(kx[:, 1, :], kfull, H2, ALU.mult, ["const"], [kk_])
                    self.tt(kx[:, 2, :], qfull, H2, ALU.mult, ["const"], [kk_])
                    kxs.append((kx, kk_))
                return kxs

            def stage_a(t, kxs):
                res = []
                for hl in range(2):
                    r0 = 64 * hl
                    ba = 4 + st["ab"] % 4
                    st["ab"] += 1
                    for d in range(2):
                        kx, kk_ = kxs[d]
                        po = self.pb(ba)[:, d * 128:(d + 1) * 128]
                        self.mm(po, kx[r0:r0 + 64, 0, :], QTL[d][r0:r0 + 64, t * 128:(t + 1) * 128], True, False, [kk_],
                                [("ps", ba)])
                        self.mm(po, kx[r0:r0 + 64, 1, :], kx[r0:r0 + 64, 2, :], False, True, [kk_], [("ps", ba)])
                    am = AM[st["am"] % 4]
                    ak = ("am", st["am"] % 4)
                    st["am"] += 1
                    self.tt(am[:, :, :], self.pb(ba, 256).rearrange("p (d c) -> p d c", d=2), MASK2, ALU.mult, ["const"],
                            [("ps", ba), ak])
                    res.append((am, ak))
                return res

            def stage_b(t, res, gi, i, nn):
                for hl in range(2):
                    r0 = 64 * hl
                    bo = 2 * (gi % 2) + hl
                    am, ak = res[hl]
                    po = self.ps[0:64, bo * 512 + i * 128:bo * 512 + (i + 1) * 128]
                    for d in range(2):
                        self.mm(po, VTM[:, t, r0:r0 + 64], am[:, d, :], d == 0, False, ["vtm", ak], [("ps", bo)])
                    for j in range(2):
                        n = 2 * t + j
                        for d in range(2):
                            self.mm(po[:, 64 * j:64 * j + 64], ST[d][r0:r0 + 64, n, r0:r0 + 64],
                                    QTL[d][r0:r0 + 64, n * 64:(n + 1) * 64], False, (j == 1 and d == 1), [], [("ps", bo)])
                if i == nn - 1:
                    w = nn * 128
                    tok0 = (t - nn + 1) * 128
                    for hl in range(2):
                        bo = 2 * (gi % 2) + hl
                        OB = G[2][0:64, 512 * hl:512 * hl + 512]
                        obk = ("ob", hl)
                        pov = self.ps[0:64, bo * 512:bo * 512 + w]
                        self.act(OB[:, 0:w], pov, AF.Copy, [], [("ps", bo), obk])
                        self.act(SQs[hl][:, 0:w], OB[:, 0:w], AF.Square, [obk], [("osq", hl)])

                    def part2(w=w, tok0=tok0):
                        for hl in range(2):
                            h = 2 * hp + hl
                            OB = G[2][0:64, 512 * hl:512 * hl + 512]
                            RSO = G[0][0:64, 512 * hl:512 * hl + 512]
                            obk, rsk = ("ob", hl), ("rso", hl)
                            bn = 4 + st["ab"] % 4
                            st["ab"] += 1
                            self.mm(self.ps[0:64, bn * 512:bn * 512 + w], self.MATS[0:64, 2, 0:64], SQs[hl][:, 0:w], True, True,
                                    [("osq", hl), "const"], [("ps", bn)])
                            self.act(RSO[:, 0:w], self.ps[0:64, bn * 512:bn * 512 + w], AF.Identity, ["const"],
                                     [("ps", bn), rsk], scale=1.0 / 64, bias=self.EPS_T[0:64, 0:1])
                            self.act(RSO[:, 0:w], RSO[:, 0:w], AF.Ln, [], [rsk])
                            self.act(RSO[:, 0:w], RSO[:, 0:w], AF.Exp, [], [rsk], scale=-0.5)
                            self.tt(OB[:, 0:w], OB[:, 0:w], RSO[:, 0:w], ALU.mult, [rsk], [obk])
                            sg, sk = self.stage(BF16, 512, 64)
                            self.stt(sg[:, 0:w], OB[:, 0:w], self.HNW[0:64, li:li + 1], OGs[hl][0:64, tok0:tok0 + w], ALU.mult,
                                     ALU.mult, [obk, ("og", hl), "const"], [sk])
                            self.dma(self.MIX[512 + 64 * h:512 + 64 * h + 64, tok0:tok0 + w], sg[:, 0:w], [sk], ["MIX"])

                    ep_pending.append(part2)

            sched = []
            for gi, t4 in enumerate(range(t_lo, NT, 4)):
                nn = min(4, NT - t4)
                for i in range(nn):
                    sched.append((t4 + i, gi, i, nn))
            ns = len(sched)
            kxq = [stage_kx(sched[0][0])]
            if ns > 1:
                kxq.append(stage_kx(sched[1][0]))
            nxt = stage_a(sched[0][0], kxq.pop(0))
            ep_pending = []
            for idx, (t, gi, i, nn) in enumerate(sched):
                cur = nxt
                if idx + 2 < ns:
                    kxq.append(stage_kx(sched[idx + 2][0]))
                nxt = stage_a(sched[idx + 1][0], kxq.pop(0)) if idx + 1 < ns else None
                run_now = list(ep_pending)
                ep_pending.clear()
                stage_b(t, cur, gi, i, nn)
                for f_ in run_now:
                    f_()
            for f_ in ep_pending:
                f_()
            self.S.barrier()
        self.stg_limit = 6

    def phase_fnet_a(self, li):
        A = self.alloc
        WF = A([128, 2, 256], BF16, "wf")
        WCS = A([128, 2, 512], BF16, "wcs")
        UFT = A([128, 2, TT], BF16, "uft")
        V12C = A([128, 2, 512], BF16, "v12c")
        self.dma(WF[:, :, :], self.wf_d[li], [], ["wf"], q="pool")
        self.dma(UFT[:, :, :], self.UF.rearrange("(k p) t -> p k t", p=128), ["UF"], ["uft"])
        for which in range(2):
            for mo in range(2):
                b = self.bank()
                for kc in range(2):
                    self.mm(self.pb(b, 256), self.DFT64[:, which, kc, mo * 128:(mo + 1) * 128], WF[:, kc, :], kc == 0,
                            kc == 1, ["const", "wf"], [("ps", b)])
                self.copy(WCS[:, mo, which * 256:(which + 1) * 256], self.pb(b, 256), [], [("ps", b), "wcs"])
        t_lo = 2 if self.last else 0
        for t in range(t_lo, NT):
            b = self.bank()
            for kc in range(2):
                self.mm(self.pb(b), UFT[:, kc, t * 128:(t + 1) * 128], WCS[:, kc, :], kc == 0, kc == 1, ["uft", "wcs"],
                        [("ps", b)])
            if t < 2:
                self.copy(V12C[:, t, :], self.pb(b), [], [("ps", b), "v12c"])
            else:
                sg, sk = self.stage(BF16, 512)
                if t % 2 == 0:
                    self.copy(sg, self.pb(b), [], [("ps", b), sk])
                else:
                    self.act(sg, self.pb(b), AF.Copy, [], [("ps", b), sk])
                self.dma(self.V12D[t * 128:(t + 1) * 128, :], sg, [sk], ["V12D"])
        if not self.last:
            for m in range(2):
                b = self.bank()
                for tt_ in range(2):
                    for which in range(2):
                        self.mm(self.pb(b, CL), V12C[:, tt_, which * 256 + m * 128:which * 256 + (m + 1) * 128],
                                self.DFTC[:, which, tt_, :], tt_ == 0 and which == 0, tt_ == 1 and which == 1,
                                ["v12c", "const"], [("ps", b)])
                sg, sk = self.stage(BF16, CL)
                self.copy(sg, self.pb(b, CL), [], [("ps", b), sk])
                self.dma(self.MIX[768 + m * 128:768 + (m + 1) * 128, 0:CL], sg, [sk], ["MIX"])

    def fnet_begin(self, half, V12T):
        self.fn = {"half": half, "dma": 0, "mm": 0, "V12T": V12T}
        self.fnet_dma()

    def fnet_dma(self):
        f = self.fn
        tt_ = f["dma"]
        if tt_ >= 16:
            return
        f["dma"] += 1
        i = tt_ % 2
        for which in range(2):
            db = self.STG[2 + 2 * which + i][:, :].bitcast(BF16)
            self.dma(db, self.k_dftL_d[which, :, tt_, f["half"] * 1024:(f["half"] + 1) * 1024], [],
                     [("fdb", which, i)], q="pool")
        self.dma(f["V12T"][i][:, :], self.V12D[(2 + tt_) * 128:(3 + tt_) * 128, :], ["V12D"], [("v12t", i)])

    def fnet_step(self):
        f = self.fn
        tt_ = f["mm"]
        if tt_ >= 16:
            return
        f["mm"] += 1
        self.fnet_dma()
        i = tt_ % 2
        v12 = f["V12T"][i]
        for m in range(2):
            for tc in range(2):
                b = 4 + m * 2 + tc
                for which in range(2):
                    db = self.STG[2 + 2 * which + i][:, :].bitcast(BF16)
                    self.mm(self.pb(b), v12[:, which * 256 + m * 128:which * 256 + (m + 1) * 128],
                            db[:, tc * 512:(tc + 1) * 512], tt_ == 0 and which == 0, tt_ == 15 and which == 1,
                            [("v12t", i), ("fdb", which, i)], [("ps", b)])
        if tt_ == 15:
            for m in range(2):
                for tc in range(2):
                    b = 4 + m * 2 + tc
                    sg, sk = self.stage(BF16, 512)
                    self.copy(sg, self.pb(b), [], [("ps", b), sk])
                    c0 = CL + f["half"] * 1024 + tc * 512
                    self.dma(self.MIX[768 + m * 128:768 + (m + 1) * 128, c0:c0 + 512], sg, [sk], ["MIX"])

    def fnet_finish(self):
        while self.fn["mm"] < 16:
            self.fnet_step()

    def phase_att(self, li):
        A = self.alloc
        QT = [A([128, TT], BF16, f"qt{i}") for i in range(2)]
        KT = [A([128, TT], BF16, f"kt{i}") for i in range(2)]
        VH = [A([128, NT, 128], BF16, f"vh{i}") for i in range(2)]
        NPB = 8
        PB = [A([128, 512], BF16, f"pb{i}") for i in range(NPB)]
        EV = [[A([128, 512], F32, f"ev{i}{j}") for j in range(4)] for i in range(2)]
        DD = A([128, 512], F32, "dd")
        SQ = A([128, 512], BF16, "asq")
        RS = A([128, 512], F32, "ars")
        items = []
        jobs = []
        for h in range(4):
            i = h % 2
            jl = [(CL + 512 * qc, 512, 0, NT) for qc in range(4)]
            if not self.last:
                jl.append((0, CL, 0, 2))
            for (q0, w, kt0, kt1) in jl:
                jobs.append((h, q0, w, kt0, kt1))
                for kti in range(kt0, kt1):
                    items.append((len(jobs) - 1, kti))
        loaded = set()
        state = {"pr": 0, "sb": 0}

        def load_head(h):
            if h in loaded or h >= 4:
                return
            loaded.add(h)
            i = h % 2
            self.dma(QT[i][:, :], self.QK[h * 128:(h + 1) * 128, :], ["QK"], [("qt", i)])
            self.dma(KT[i][:, :], self.QK[512 + h * 128:512 + (h + 1) * 128, :], ["QK"], [("kt", i)])
            self.dma(VH[i][:, :, :], self.VT[:, h * 128:(h + 1) * 128].rearrange("(t p) c -> p t c", p=128), ["VT"],
                     [("vh", i)])

        def stage_s(item):
            jb, kti = item
            h, q0, w, kt0, kt1 = jobs[jb]
            load_head(h)
            i = h % 2
            out = []
            bss = []
            for c in range(2):
                bs = 4 + state["sb"] % 4
                state["sb"] += 1
                bss.append(bs)
                self.mm(self.pb(bs, w), KT[i][64 * c:64 * c + 64, kti * 128:(kti + 1) * 128],
                        QT[i][64 * c:64 * c + 64, q0:q0 + w], True, True, [("kt", i), ("qt", i)], [("ps", bs)])
            for c in range(2):
                p = PB[state["pr"] % NPB]
                pk = ("pb", state["pr"] % NPB)
                state["pr"] += 1
                self.act(p[:, 0:w], self.pb(bss[c], w), AF.Exp, [], [("ps", bss[c]), pk], scale=0.125)
                out.append((p, pk))
            return out

        njob = 0
        pending = []

        def sbank():
            b = 4 + state["sb"] % 4
            state["sb"] += 1
            return b

        nxt = stage_s(items[0])
        for idx, item in enumerate(items):
            cur = nxt
            nxt = stage_s(items[idx + 1]) if idx + 1 < len(items) else None
            jb, kti = item
            h, q0, w, kt0, kt1 = jobs[jb]
            i = h % 2
            for c in range(2):
                p, pk = cur[c]
                self.mm(self.pb(2 * c, w), VH[i][:, kti, :], p[:, 0:w], kti == kt0, kti == kt1 - 1, [("vh", i), pk],
                        [("ps", 2 * c)])
                self.mm(self.pb(2 * c + 1, w), self.ONES, p[:, 0:w], kti == kt0, kti == kt1 - 1, ["const", pk],
                        [("ps", 2 * c + 1)])
            load_head(h + 1)
            if li + 1 < self.nlayers and kti % 6 == 3:
                self.mod_pump(li + 1, 1, sbank)
            if kti == kt1 - 1:
                for pe_ in pending:
                    pe_[1]()
                pending.clear()
                ev = EV[njob % 2]
                ek = [("ev", njob % 2, x) for x in range(4)]
                njob += 1
                for x in range(4):
                    self.copy(ev[x][:, 0:w], self.pb(x, w), [], [("ps", x), ek[x]])
                for x in (1, 3):
                    self.act(ev[x][:, 0:w], ev[x][:, 0:w], AF.Ln, [], [ek[x]])
                    self.act(ev[x][:, 0:w], ev[x][:, 0:w], AF.Exp, [], [ek[x]], scale=-1.0)
                for c in range(2):
                    self.tt(ev[2 * c][:, 0:w], ev[2 * c][:, 0:w], ev[2 * c + 1][:, 0:w], ALU.mult, [ek[2 * c + 1]],
                            [ek[2 * c]])
                self.stt(DD[:, 0:w], ev[2][:, 0:w], self.SM[:, 2:3], ev[0][:, 0:w], ALU.mult, ALU.add,
                         [ek[0], ek[2], "sm"], ["dd"])
                self.tt(SQ[:, 0:w], DD[:, 0:w], DD[:, 0:w], ALU.mult, ["dd"], ["asq"])

                def part2(h=h, q0=q0, w=w):
                    bn = 4 + state["sb"] % 4
                    state["sb"] += 1
                    self.mm(self.pb(bn, w), self.ONES, SQ[:, 0:w], True, True, ["asq", "const"], [("ps", bn)])
                    self.act(RS[:, 0:w], self.pb(bn, w), AF.Identity, ["const"], [("ps", bn), "ars"], scale=1.0 / 128,
                             bias=self.EPS_T[:, 0:1])
                    self.act(RS[:, 0:w], RS[:, 0:w], AF.Ln, [], ["ars"])
                    self.act(RS[:, 0:w], RS[:, 0:w], AF.Exp, [], ["ars"], scale=-0.5)
                    sg, sk = self.stage(BF16, w)
                    self.stt(sg, DD[:, 0:w], self.SM[:, 3:4], RS[:, 0:w], ALU.mult, ALU.mult, ["dd", "ars", "sm"], [sk])
                    self.dma(self.MIX[h * 128:(h + 1) * 128, q0:q0 + w], sg, [sk], ["MIX"])

                pending.append([6, part2])
            for pe_ in list(pending):
                pe_[0] -= 1
                if pe_[0] <= 0:
                    pe_[1]()
                    pending.remove(pe_)
        for pe_ in pending:
            pe_[1]()

    def phase_wout(self, li):
        grps = self.groups(ctx=not self.last)
        T_ = sum(g[1] for g in grps)
        H2 = self.alloc([128, 8, T_], BF16, "h2")
        SQ = self.alloc([128, 8, 512], BF16, "sq")
        RS = self.alloc([128, 512], F32, "rs")
        T1 = [self.alloc([128, 512], F32, f"t1{i}") for i in range(2)]
        WO = self.alloc([128, 8, D], BF16, "wo")
        for pc in range(4):
            self.dma(WO[:, :, pc * 256:(pc + 1) * 256], self.wo_d[li, :, :, pc * 256:(pc + 1) * 256], [], [("wo", pc)],
                     q="pool")
        MG = [self.alloc([128, 8, 512], BF16, f"mg{i}") for i in range(2)]
        hoff = 0
        norm_pending = []
        wu0 = None
        for gi, (tok0, w, n) in enumerate(grps):
            mg = MG[gi % 2]
            mk = ("mg", gi % 2)
            self.dma(mg[:, :, 0:w], self.MIX[:, tok0:tok0 + w].rearrange("(k p) t -> p k t", p=128), ["MIX"], [mk])
            for m in range(8):
                b = self.bank()
                for k in range(8):
                    self.mm(self.pb(b, w), WO[:, k, m * 128:(m + 1) * 128], mg[:, k, 0:w], k == 0, k == 7,
                            [("wo", m // 2), mk], [("ps", b)])
                self.stt(self.X[:, m, tok0:tok0 + w], self.pb(b, w), self.MOD[:, li, 16 + m, n:n + 1],
                         self.X[:, m, tok0:tok0 + w], ALU.mult, ALU.add, ["mod"], [("ps", b), ("X", gi)])
            for f_ in norm_pending:
                f_()
            norm_pending.clear()
            norm_pending.append(lambda tok0=tok0, w=w, n=n, hoff=hoff, gi=gi: self.norm_mod(
                tok0, w, n, 1, H2[:, :, hoff:hoff + w], "h2", (SQ, RS, T1), xkey=("X", gi)))
            hoff += w
        save = self.off
        self.off = self.offs["stg0"]
        wu0 = self.alloc([128, 8, 256], BF16, "wu0pre")
        self.off = save
        self.dma(wu0[:, :, 0:128], self.wu_d[li, :, :, 0:128], [], [("wupre", 0)], q="pool")
        self.dma(wu0[:, :, 128:256], self.wu_d[li, :, :, DFF:DFF + 128], [], [("wupre", 0)], q="pool")
        for f_ in norm_pending:
            f_()

    def phase_ffn(self, li):
        A = self.alloc
        if self.last:
            chunks = [(CL + 512 * i, 512, 0, 512 * i) for i in range(4)]
            TC = L
        else:
            chunks = [(0, CL, 1, 0)] + [(CL + 512 * i, 512, 0, CL + 2 + 512 * i) for i in range(4)]
            TC = TT + 2
        gsz = [4, 4, 4, 4, 3, 3]
        T_ = sum(c[1] for c in chunks)
        H2 = A([128, 8, T_], BF16, "h2")
        hoff = []
        o = 0
        for (x0, w, n, co) in chunks:
            hoff.append(o)
            o += w
        ACTB = A([128, 4, TC], BF16, "actb")
        CC = [A([128, TC], F32, f"cc{i}") for i in range(2)]
        UU = [A([128, TC + 2], F32, f"uu{i}") for i in range(2)]
        WD = [A([128, 4, 128], BF16, f"wd{i}") for i in range(2)]
        save = self.off
        self.off = self.offs["stg0"]
        WU = [A([128, 8, 256], BF16, f"wu{i}") for i in range(2)]
        assert self.off <= self.offs["stg0"] + 6 * 2048
        self.off = save
        for i in range(2):
            self.memset(UU[i][:, :], 0.0, [("uu", i)])
        wr = [0]

        def pair_a(j):
            wu = WU[j % 2]
            wk = ("wu", j % 2)
            if j > 0:
                self.dma(wu[:, :, 0:128], self.wu_d[li, :, :, j * 128:(j + 1) * 128], [], [wk], q="pool")
                self.dma(wu[:, :, 128:256], self.wu_d[li, :, :, DFF + j * 128:DFF + (j + 1) * 128], [], [wk], q="pool")
            for part in range(2):
                ci = j if part == 0 else NPAIR + j
                cc = CC[part]
                ck = ("cc", part)
                uu = UU[part]
                uk = ("uu", part)
                for cidx, (x0, w, n, co) in enumerate(chunks):
                    b = self.bank()
                    for k in range(8):
                        self.mm(self.pb(b, w), wu[:, k, part * 128:(part + 1) * 128],
                                H2[:, k, hoff[cidx]:hoff[cidx] + w], k == 0, k == 7, [wk, "h2"], [("ps", b)])
                    self.act(uu[:, 1 + co:1 + co + w], self.pb(b, w), AF.Copy, [], [("ps", b), uk])
                self.act(cc[:, :], uu[:, 1:TC + 1], AF.Identity, ["const", uk], [ck], scale=self.CW[:, li, ci, 1:2],
                         bias=self.CB[:, li, ci:ci + 1])
                self.stt(cc[:, :], uu[:, 0:TC], self.CW[:, li, ci, 0:1], cc[:, :], ALU.mult, ALU.add,
                         ["const", uk], [ck])
                self.stt(cc[:, :], uu[:, 2:TC + 2], self.CW[:, li, ci, 2:3], cc[:, :], ALU.mult, ALU.add,
                         ["const", uk], [ck])
            self.act(CC[0][:, :], CC[0][:, :], AF.Silu, [], [("cc", 0)])

        def pair_b(jj):
            self.tt(ACTB[:, jj, :], CC[0][:, :], CC[1][:, :], ALU.mult, [("cc", 0), ("cc", 1)], [("actb", jj)])

        def down(jbase, gs):
            for m in range(8):
                wd = WD[wr[0] % 2]
                dk = ("wd", wr[0] % 2)
                wr[0] += 1
                self.dma(wd[:, 0:gs, :], self.wd_d[li, :, jbase:jbase + gs, m * 128:(m + 1) * 128], [], [dk], q="pool")
                for (x0, w, n, co) in chunks:
                    b = self.bank()
                    for jj in range(gs):
                        self.mm(self.pb(b, w), wd[:, jj, :], ACTB[:, jj, co:co + w], jj == 0, jj == gs - 1,
                                [dk, ("actb", jj)], [("ps", b)])
                    xs = self.X[:, m, x0:x0 + w]
                    self.stt(xs, self.pb(b, w), self.MOD[:, li, 40 + m, n:n + 1], xs, ALU.mult, ALU.add, ["mod"],
                             [("ps", b), "X"])

        def down_final(jbase, gs):
            WDL = A([128, 8, gs, 128], BF16, "wdl")
            for m in range(8):
                self.dma(WDL[:, m, :, :], self.wd_d[li, :, jbase:jbase + gs, m * 128:(m + 1) * 128], [], [("wdl", m)],
                         q="pool")
            SQv = UU[0][:, 0:2048].bitcast(BF16).rearrange("p (k t) -> p k t", k=8)
            RSv = UU[1][:, 0:512]
            pend = []

            def fin(ci, x0):
                self.rstd_bcast(self.X[:, :, x0:x0 + 512], 512, RSv, ("uu", 1), [("Xf", ci)], SQv, ("uu", 0))
                for k in range(8):
                    i = 4 + k % 2
                    sg = self.STG[i][:, 0:512]
                    self.stt(sg, self.X[:, k, x0:x0 + 512], self.FNW[:, k:k + 1], RSv, ALU.mult, ALU.mult,
                             [("Xf", ci), ("uu", 1), "const"], [("stg", i)])
                    self.dma(self.outT_d[:, k, x0 - CL:x0 - CL + 512], sg, [("stg", i)], ["outT"])

            for ci, (x0, w, n, co) in enumerate(chunks):
                for m in range(8):
                    b = self.bank()
                    for jj in range(gs):
                        self.mm(self.pb(b, w), WDL[:, m, jj, :], ACTB[:, jj, co:co + w], jj == 0, jj == gs - 1,
                                [("wdl", m), ("actb", jj)], [("ps", b)])
                    xs = self.X[:, m, x0:x0 + w]
                    self.stt(xs, self.pb(b, w), self.MOD[:, li, 40 + m, n:n + 1], xs, ALU.mult, ALU.add, ["mod"],
                             [("ps", b), ("Xf", ci)])
                for f_ in pend:
                    f_()
                pend = [lambda ci=ci, x0=x0: fin(ci, x0)]
            for f_ in pend:
                f_()
            self.final_fused = True

        j = 0
        prev = None
        for grp in range(len(gsz)):
            jbase = j
            for jj in range(gsz[grp]):
                pair_a(j)
                if jj == 0 and prev is not None:
                    down(*prev)
                    prev = None
                pair_b(jj)
                j += 1
            prev = (jbase, gsz[grp])
        if self.last and self.stop is None and not self.dbg:
            down_final(*prev)
        else:
            down(*prev)
        self.S.barrier()

    def final_norm(self):
        self.off = self.persist_mark
        SQ = self.alloc([128, 8, 512], BF16, "sq")
        RS = self.alloc([128, 512], F32, "rs")
        OUT = [self.alloc([128, 8, 512], F32, f"out{i}") for i in range(2)]
        for g in range(4):
            tok0 = CL + 512 * g
            self.rstd_bcast(self.X[:, :, tok0:tok0 + 512], 512, RS, "rs", ["X"], SQ, "sq")
            o = OUT[g % 2]
            ok = ("out", g % 2)
            for k in range(8):
                self.stt(o[:, k, :], self.X[:, k, tok0:tok0 + 512], self.FNW[:, k:k + 1], RS[:, :], ALU.mult, ALU.mult,
                         ["X", "rs", "const"], [ok])
            self.dma(self.outT_d[:, :, 512 * g:512 * (g + 1)], o[:, :, :], [ok], ["outT"])

    def dump_x(self):
        self.S.barrier()
        self.dma(self.MODD[:, :, :, :], self.MOD[:, :, :, :], [], [])
        self.dma(self.XD, self.X[:, :, :], ["X"], [])


def _pk(a):
    k = a.shape[0] // 128
    return np.ascontiguousarray(a.reshape(k, 128, -1).transpose(1, 0, 2))


def _vec(a):
    return np.ascontiguousarray(a.reshape(-1, 128).T)


_CONST_CACHE = {}


def host_consts():
    if _CONST_CACHE:
        return _CONST_CACHE
    f32 = np.float32
    half = 32
    inv_freq = (10000.0 ** (-np.arange(0, half, 2, dtype=np.float64) / half))
    t = np.arange(L)
    pos_r = (t // 64).astype(np.float64)
    pos_c = (t % 64).astype(np.float64)
    ang_r = pos_r[:, None] * inv_freq
    ang_c = pos_c[:, None] * inv_freq
    ang = np.concatenate([ang_r, ang_r, ang_c, ang_c], axis=-1)
    cosT = np.cos(ang).T
    sinT = np.sin(ang).T
    rope = np.zeros((128, 2, L), f32)
    rope[:, 0] = np.concatenate([cosT, cosT], 0)
    rope[:, 1] = np.concatenate([sinT, sinT], 0)
    mats = np.zeros((128, 7, 128), f32)
    mats[:, 5] = ((np.arange(128) % 64) < 32)[None, :]
    mats[:, 6] = ((np.arange(128) % 64) >= 32)[None, :]
    mats[:, 0] = np.eye(128)
    R = np.zeros((128, 128))
    for dp in range(128):
        d = dp % 64
        base = dp - d
        if d % 32 < 16:
            R[base + d + 16, dp] = -1.0
        else:
            R[base + d - 16, dp] = 1.0
    mats[:, 1] = R
    mats[:, 2] = 1.0
    s = np.arange(128)[:, None]
    tt = np.arange(128)[None, :]
    same = (s // 64) == (tt // 64)
    mats[:, 3] = (same & (s <= tt))
    mats[:, 4] = (same & (s >= tt))
    scan = np.ones((128, TT), f32)
    scan[:, ::64] = 0.0
    i64 = np.arange(64)
    c64 = np.cos(2 * np.pi * np.outer(i64, i64) / 64) / 8.0
    s64 = np.sin(2 * np.pi * np.outer(i64, i64) / 64) / 8.0
    cbd = np.kron(np.eye(4), c64)
    sbd = np.kron(np.eye(4), s64)
    dft64 = np.stack([_pk(cbd), _pk(sbd)], axis=1).astype(f32)

    def dft(n):
        idx = np.arange(n, dtype=np.int64)
        m = (np.outer(idx, idx) % n).astype(np.float64) * (2 * np.pi / n)
        return (np.cos(m) / math.sqrt(n)).astype(f32), (-np.sin(m) / math.sqrt(n)).astype(f32)

    cL, sL = dft(L)
    dftL = np.stack([_pk(cL), _pk(sL)], axis=0)
    cC, sC = dft(CL)
    dftC = np.stack([_pk(cC), _pk(sC)], axis=1)
    _CONST_CACHE.update(k_rope=rope, k_mats=mats, k_scan=scan, k_dft64=np.ascontiguousarray(dft64),
                        k_dftL=np.ascontiguousarray(dftL), k_dftC=np.ascontiguousarray(dftC))
    return _CONST_CACHE


def host_inputs(x, c, ctx, c_ctx, w_ada, b_ada, norm1_w, norm2_w, w_in, lam_qk, subln_w, lb_param, hgrn_norm_w, w_fnet,
                w_out, w_up, conv_w, conv_b, w_down, final_norm_w, cores=range(8)):
    f = lambda a: np.ascontiguousarray(np.asarray(a, dtype=np.float32))
    x, c, ctx, c_ctx = f(x), f(c), f(ctx), f(c_ctx)
    shared = dict(host_consts())
    shared["w_ada"] = np.stack([_pk(f(w_ada[l])) for l in range(2)])
    shared["b_ada"] = np.ascontiguousarray(np.stack([_vec(f(b_ada[l])) for l in range(2)], axis=1))
    shared["norm1_w"] = np.ascontiguousarray(np.stack([_vec(f(norm1_w[l])) for l in range(2)], axis=1))
    shared["norm2_w"] = np.ascontiguousarray(np.stack([_vec(f(norm2_w[l])) for l in range(2)], axis=1))
    shared["final_norm_w"] = _vec(f(final_norm_w))
    shared["w_in"] = np.stack([_pk(f(w_in[l])) for l in range(2)])
    lam = f(lam_qk).reshape(2, 256)
    shared["lam_qk"] = np.ascontiguousarray(np.broadcast_to(lam[None], (128, 2, 256)))
    shared["subln_w"] = np.ascontiguousarray(f(subln_w).T)
    lb = f(lb_param).reshape(2, 2, 2, 128)
    shared["lb_param"] = np.ascontiguousarray(lb.transpose(3, 0, 1, 2))
    hn = f(hgrn_norm_w)
    shared["hgrn_norm_w"] = np.ascontiguousarray(np.concatenate([hn, hn], axis=1).T)
    shared["w_fnet"] = np.stack([_pk(f(w_fnet[l])) for l in range(2)])
    shared["w_out"] = np.stack([_pk(f(w_out[l])) for l in range(2)])
    shared["w_up"] = np.stack([_pk(f(w_up[l])) for l in range(2)])
    cw = f(conv_w)
    shared["conv_w"] = np.ascontiguousarray(cw.reshape(2, 3, 44, 128).transpose(3, 0, 2, 1))
    shared["conv_b"] = np.ascontiguousarray(f(conv_b).reshape(2, 44, 128).transpose(2, 0, 1))
    shared["w_down"] = np.stack([_pk(f(w_down[l])) for l in range(2)])
    maps = []
    for b in cores:
        m = dict(shared)
        m["xT"] = _pk(np.ascontiguousarray(x[b].T))
        m["ctxT"] = _pk(np.ascontiguousarray(ctx[b].T))
        m["cvec"] = np.ascontiguousarray(np.stack([_vec(c[b]), _vec(c_ctx)], axis=-1))
        maps.append(m)
    return maps


_NC_CACHE = {}


def kernel(**inputs):
    maps = host_inputs(**inputs)
    if "nc" not in _NC_CACHE:
        _NC_CACHE["nc"] = Builder().build()
    res = run_bass_kernel_spmd(_NC_CACHE["nc"], maps, core_ids=list(range(8)))
    out = np.empty((8, L, D), np.float32)
    for b in range(8):
        oT = np.asarray(res.results[b]["outT"])
        out[b] = oT.transpose(2, 1, 0).reshape(L, D)
    return out
```

```python
import math
from contextlib import ExitStack
import numpy as np
import concourse.bass as bass
import concourse.mybir as mybir
from concourse.bass_utils import run_bass_kernel_spmd

F32 = mybir.dt.float32
BF16 = mybir.dt.bfloat16
AF = mybir.ActivationFunctionType
ALU = mybir.AluOpType
AX = mybir.AxisListType

ENGS = ("pe", "act", "dve", "pool", "sp")
EPOCH = 24000
NDSEM = 8
SB_BASE = 17408
SB_LIMIT = 229376

D = 1024
L = 2048
CL = 256
TT = CL + L
NT = TT // 128
NCH = TT // 64
DFF = 2816
NPAIR = 22
EPS = 1e-6


class Sched:
    def __init__(self):
        self.ops = {e: [] for e in ENGS}
        self.cnt = {e: 0 for e in ENGS}
        self.last_w = {}
        self.readers = {}
        self.waited = {e: {} for e in ENGS}
        self.ndma = {"sp": 0, "pool": 0}
        self.all_tokens = {}
        self.semkeys = set()

    def _need(self, eng, tok, waits):
        sk, val = tok
        if sk[0] == "pe" and eng == "pe":
            return
        if self.waited[eng].get(sk, 0) >= val:
            return
        self.waited[eng][sk] = val
        waits.append((sk, val))

    def _deps(self, eng, reads, writes):
        waits = []
        for k in reads:
            w = self.last_w.get(k)
            if w is not None:
                self._need(eng, w, waits)
        for k in writes:
            w = self.last_w.get(k)
            if w is not None:
                self._need(eng, w, waits)
            for r in self.readers.get(k, ()):
                self._need(eng, r, waits)
        return waits

    def _commit(self, tok, reads, writes):
        for k in reads:
            self.readers.setdefault(k, []).append(tok)
        for k in writes:
            self.last_w[k] = tok
            self.readers[k] = []
        sk, val = tok
        self.all_tokens[sk] = max(self.all_tokens.get(sk, 0), val)
        self.semkeys.add(sk)

    def op(self, eng, fn, reads=(), writes=()):
        waits = self._deps(eng, reads, writes)
        self.cnt[eng] += 1
        c = self.cnt[eng]
        sk = (eng, (c - 1) // EPOCH)
        tok = (sk, (c - 1) % EPOCH + 1)
        self.ops[eng].append((waits, fn, (sk, 1)))
        self._commit(tok, reads, writes)
        return tok

    def dma(self, q, fn, reads=(), writes=()):
        waits = self._deps(q, reads, writes)
        j = self.ndma[q]
        self.ndma[q] += 1
        sk = ("dma" + q, j % NDSEM)
        val = 16 * (j // NDSEM + 1)
        if val > 16:
            self._need(q, (sk, val - 16), waits)
        tok = (sk, val)
        self.ops[q].append((waits, fn, (sk, 16)))
        self._commit(tok, reads, writes)
        return tok

    def barrier(self):
        for e in ENGS:
            waits = []
            for sk, val in self.all_tokens.items():
                self._need(e, (sk, val), waits)
            if waits:
                self.ops[e].append((waits, None, None))
        self.last_w = {}
        self.readers = {}

    def final_wait(self, eng="sp"):
        waits = []
        for sk, val in self.all_tokens.items():
            self._need(eng, (sk, val), waits)
        if waits:
            self.ops[eng].append((waits, None, None))

    def emit(self, nc, stack):
        sems = {}
        for sk in sorted(self.semkeys, key=str):
            sems[sk] = stack.enter_context(nc.semaphore("s_" + "_".join(str(x) for x in sk)))
        block = stack.enter_context(nc.Block())

        def run(engname, e):
            for waits, fn, inc in self.ops[engname]:
                for sk, val in waits:
                    e.wait_ge(sems[sk], val)
                if fn is not None:
                    fn(e).then_inc(sems[inc[0]], inc[1])

        @block.tensor
        def _(e):
            run("pe", e)

        @block.scalar
        def _(e):
            run("act", e)

        @block.vector
        def _(e):
            run("dve", e)

        @block.gpsimd
        def _(e):
            run("pool", e)

        @block.sync
        def _(e):
            run("sp", e)


def _dsize(dt):
    return 2 if dt == BF16 else 4


class Builder:
    def __init__(self, dbg=False, nlayers=2, stop=None):
        self.dbg = dbg
        self.nlayers = nlayers
        self.stop = stop
        self.nc = bass.Bass("TRN2", target_bir_lowering=False)
        self.S = Sched()
        self.off = SB_BASE
        self.nalloc = 0
        self.bank_rr = 0
        self.stg_rr = 0
        self.stg_limit = 6
        self.din = {}
        self.offs = {}

    def alloc(self, shape, dt, name="t"):
        nb = int(np.prod(shape[1:])) * _dsize(dt)
        nb = (nb + 31) // 32 * 32
        self.nalloc += 1
        h = self.nc.alloc_sbuf_tensor_at(f"{name}_{self.nalloc}", list(shape), dt, offset=self.off)
        self.offs[name] = self.off
        self.off += nb
        assert self.off <= SB_LIMIT, (name, self.off)
        return h

    def inp(self, name, shape):
        t = self.nc.dram_tensor(name, list(shape), F32, kind="ExternalInput").ap()
        self.din[name] = t
        return t

    def scratch(self, name, shape, dt):
        kind = "ExternalOutput" if self.dbg else "Internal"
        return self.nc.dram_tensor(name, list(shape), dt, kind=kind).ap()

    def bank(self, lo=0, hi=8):
        b = lo + self.bank_rr % (hi - lo)
        self.bank_rr += 1
        return b

    def pb(self, b, w=512, p=128):
        return self.ps[0:p, b * 512:b * 512 + w]

    def mm(self, out, lhsT, rhs, start, stop, reads, writes):
        self.S.op("pe", lambda e: e.matmul(out, lhsT=lhsT, rhs=rhs, start=start, stop=stop), reads, writes)

    def tr(self, out, in_, reads, writes):
        idn = self.IDN
        self.S.op("pe", lambda e: e.transpose(out=out, in_=in_, identity=idn[:, :]), list(reads) + ["const"], writes)

    def act(self, out, in_, func, reads, writes, scale=None, bias=None):
        kw = {}
        if scale is not None:
            kw["scale"] = scale
        if bias is not None:
            kw["bias"] = bias
        self.S.op("act", lambda e: e.activation(out=out, in_=in_, func=func, **kw), reads, writes)

    def tt(self, out, in0, in1, op, reads, writes, eng="dve"):
        self.S.op(eng, lambda e: e.tensor_tensor(out=out, in0=in0, in1=in1, op=op), reads, writes)

    def ts(self, out, in0, s1, s2, op0, op1, reads, writes, eng="dve"):
        if op1 is None:
            self.S.op(eng, lambda e: e.tensor_scalar(out=out, in0=in0, scalar1=s1, scalar2=None, op0=op0), reads, writes)
        else:
            self.S.op(eng, lambda e: e.tensor_scalar(out=out, in0=in0, scalar1=s1, scalar2=s2, op0=op0, op1=op1),
                      reads, writes)

    def stt(self, out, in0, scalar, in1, op0, op1, reads, writes):
        self.S.op("dve", lambda e: e.scalar_tensor_tensor(out=out, in0=in0, scalar=scalar, in1=in1, op0=op0, op1=op1),
                  reads, writes)

    def copy(self, out, in_, reads, writes, eng="dve"):
        self.S.op(eng, lambda e: e.tensor_copy(out=out, in_=in_), reads, writes)

    def recip(self, out, in_, reads, writes):
        self.S.op("dve", lambda e: e.reciprocal(out=out, in_=in_), reads, writes)

    def memset(self, ap, v, writes, eng="dve"):
        self.S.op(eng, lambda e: e.memset(ap, v), (), writes)

    def dma(self, out, in_, reads, writes, q="sp"):
        self.S.dma(q, lambda e: e.dma_start(out=out, in_=in_), reads, writes)

    def stage(self, dt, w=512, p=128):
        i = self.stg_rr % self.stg_limit
        self.stg_rr += 1
        t = self.STG[i]
        if dt == F32:
            ap = t[0:p, 0:w]
        else:
            ap = t[0:p, 0:w // 2].bitcast(BF16)
        return ap, ("stg", i)

    def build(self):
        nc = self.nc
        with ExitStack() as st:
            self.ps = st.enter_context(nc.psum_tensor("ps", [128, 4096], F32))
            self.declare_io()
            self.setup_consts()
            for li in range(self.nlayers):
                self.layer(li)
                if self.stop is not None and self.stop[0] == li:
                    break
            else:
                if not getattr(self, "final_fused", False):
                    self.final_norm()
            if self.dbg:
                self.dump_x()
            self.S.final_wait("sp")
            self.S.emit(nc, st)
        return nc

    def declare_io(self):
        self.xT_d = self.inp("xT", [128, 8, L])
        self.cT_d = self.inp("ctxT", [128, 8, CL])
        self.cvec_d = self.inp("cvec", [128, 8, 2])
        self.wada_d = self.inp("w_ada", [2, 128, 8, 6 * D])
        self.bada_d = self.inp("b_ada", [128, 2, 48])
        self.n1_d = self.inp("norm1_w", [128, 2, 8])
        self.n2_d = self.inp("norm2_w", [128, 2, 8])
        self.fn_d = self.inp("final_norm_w", [128, 8])
        self.win_d = self.inp("w_in", [2, 128, 8, 3072])
        self.lam_d = self.inp("lam_qk", [128, 2, 256])
        self.subw_d = self.inp("subln_w", [128, 2])
        self.lb_d = self.inp("lb_param", [128, 2, 2, 2])
        self.hnw_d = self.inp("hgrn_norm_w", [128, 2])
        self.wf_d = self.inp("w_fnet", [2, 128, 2, 256])
        self.wo_d = self.inp("w_out", [2, 128, 8, D])
        self.wu_d = self.inp("w_up", [2, 128, 8, 2 * DFF])
        self.cw_d = self.inp("conv_w", [128, 2, 44, 3])
        self.cb_d = self.inp("conv_b", [128, 2, 44])
        self.wd_d = self.inp("w_down", [2, 128, NPAIR, D])
        self.k_rope_d = self.inp("k_rope", [128, 2, L])
        self.k_mats_d = self.inp("k_mats", [128, 7, 128])
        self.k_scan_d = self.inp("k_scan", [128, TT])
        self.k_dft64_d = self.inp("k_dft64", [128, 2, 2, 256])
        self.k_dftL_d = self.inp("k_dftL", [2, 128, 16, L])
        self.k_dftC_d = self.inp("k_dftC", [128, 2, 2, CL])
        self.outT_d = self.nc.dram_tensor("outT", [128, 8, L], F32, kind="ExternalOutput").ap()
        self.QK = self.scratch("s_qk", [1024, TT], BF16)
        self.GZ = self.scratch("s_gz", [1024, TT], F32)
        self.UF = self.scratch("s_uf", [256, TT], BF16)
        self.VT = self.scratch("s_vt", [TT, 768], BF16)
        self.MIX = self.scratch("s_mix", [1024, TT], BF16)
        self.V12D = self.scratch("s_v12", [TT, 512], BF16)
        if self.dbg:
            self.XD = self.nc.dram_tensor("s_x", [128, 8, TT], F32, kind="ExternalOutput").ap()
            self.MODD = self.nc.dram_tensor("s_mod", [128, 2, 48, 2], F32, kind="ExternalOutput").ap()

    def setup_consts(self):
        A = self.alloc
        self.X = A([128, 8, TT], F32, "X")
        self.ROPE = A([128, 2, L], BF16, "rope")
        self.MATS = A([128, 7, 128], BF16, "mats")
        self.SCANM = A([128, TT], BF16, "scanm")
        self.DFT64 = A([128, 2, 2, 256], BF16, "dft64")
        self.DFTC = A([128, 2, 2, CL], BF16, "dftc")
        self.CV = A([128, 8, 2], F32, "cv")
        self.CACT = A([128, 8, 2], BF16, "cact")
        self.BADA = A([128, 2, 48], F32, "bada")
        self.MOD = A([128, 2, 48, 2], F32, "mod")
        self.N1 = A([128, 2, 8], F32, "n1")
        self.N2 = A([128, 2, 8], F32, "n2")
        self.FNW = A([128, 8], F32, "fnw")
        self.AB = A([128, 2, 2, 8, 2], F32, "ab")
        self.LAMQ = A([128, 2, 256], F32, "lamq")
        self.SUBW = A([128, 2], F32, "subw")
        self.LBP = A([128, 2, 2, 2], F32, "lbp")
        self.LB = A([128, 2, 2, 2], F32, "lb")
        self.HNW = A([128, 2], F32, "hnw")
        self.CW = A([128, 2, 44, 3], F32, "cw")
        self.CB = A([128, 2, 44], F32, "cb")
        self.EPS_T = A([128, 1], F32, "eps")
        self.ZERO_T = A([128, 1], F32, "zero")
        self.ONE_T = A([128, 1], F32, "one")
        self.SM = A([128, 16], F32, "small")
        self.STG = [A([128, 512], F32, f"stg{i}") for i in range(6)]
        self.WA = [A([128, 8, 128], BF16, f"wap{i}") for i in range(2)]
        self.mod_next = {0: 0, 1: 0}
        self.IDN = self.MATS[:, 0, :]
        self.RPM = self.MATS[:, 1, :]
        self.ONES = self.MATS[:, 2, :]
        self.MASKF = self.MATS[:, 3, :]
        self.MASKB = self.MATS[:, 4, :]
        self.HA = self.MATS[:, 5, :]
        self.HB = self.MATS[:, 6, :]
        d = self.dma
        d(self.X[:, :, 0:CL], self.cT_d, [], ["X"])
        d(self.X[:, :, CL:TT], self.xT_d, [], ["X"])
        d(self.MATS[:, :, :], self.k_mats_d, [], ["const"], q="pool")
        d(self.SCANM[:, :], self.k_scan_d, [], ["const"], q="pool")
        d(self.DFT64[:, :, :, :], self.k_dft64_d, [], ["const"], q="pool")
        d(self.DFTC[:, :, :, :], self.k_dftC_d, [], ["const"], q="pool")
        d(self.CV[:, :, :], self.cvec_d, [], ["const"])
        d(self.BADA[:, :, :], self.bada_d, [], ["const"])
        d(self.N1[:, :, :], self.n1_d, [], ["const"])
        d(self.N2[:, :, :], self.n2_d, [], ["const"])
        d(self.FNW[:, :], self.fn_d, [], ["const"])
        d(self.LAMQ[:, :, :], self.lam_d, [], ["const"])
        d(self.SUBW[:, :], self.subw_d, [], ["const"])
        d(self.LBP[:, :, :, :], self.lb_d, [], ["const"])
        d(self.HNW[:, :], self.hnw_d, [], ["const"])
        d(self.CW[:, :, :, :], self.cw_d, [], ["const"])
        d(self.CB[:, :, :], self.cb_d, [], ["const"])
        self.memset(self.EPS_T[:, :], EPS, ["const"])
        self.memset(self.ZERO_T[:, :], 0.0, ["const"])
        self.memset(self.ONE_T[:, :], 1.0, ["const"])
        self.S.barrier()
        self.act(self.CACT[:, :, :], self.CV[:, :, :], AF.Silu, ["const"], ["cact"])
        lbe = self.STG[0][:, 0:8].rearrange("p (a b c) -> p a b c", a=2, b=2)
        self.act(lbe, self.LBP[:, :, :, :], AF.Exp, ["const"], [("stg", 0)])
        den = self.STG[1][:, 0:4].rearrange("p (b c) -> p b c", b=2)
        self.tt(den, lbe[:, 0, :, :], lbe[:, 1, :, :], ALU.add, [("stg", 0)], [("stg", 1)])
        self.recip(den, den, [], [("stg", 1)])
        self.memset(self.LB[:, 0, :, :], 0.0, ["lb"])
        self.tt(self.LB[:, 1, :, :], lbe[:, 1, :, :], den, ALU.mult, [("stg", 0), ("stg", 1)], ["lb"])
        self.S.barrier()
        self.persist_mark = self.off

    def layer(self, li):
        self.li = li
        self.last = li == 1
        self.off = self.persist_mark
        self.phase_mod(li)
        if self.stop == (li, "mod"):
            return
        self.off = self.persist_mark
        self.phase_proj(li)
        self.S.barrier()
        if self.stop == (li, "proj"):
            return
        self.off = self.persist_mark
        self.phase_fnet_a(li)
        self.S.barrier()
        self.off = self.persist_mark
        self.phase_hgrn(li)
        self.S.barrier()
        if self.stop in ((li, "hgrn"), (li, "fnet")):
            return
        self.off = self.persist_mark
        self.phase_att(li)
        self.S.barrier()
        if self.stop == (li, "att"):
            return
        self.off = self.persist_mark
        self.phase_wout(li)
        self.S.barrier()
        if self.stop == (li, "wout"):
            return
        self.off = self.persist_mark
        self.phase_ffn(li)
        self.S.barrier()

    def mod_pump(self, li, npieces, bank_fn=None):
        for _ in range(npieces):
            pc = self.mod_next[li]
            if pc >= 48:
                return
            self.mod_next[li] += 1
            wa = self.WA[pc % 2]
            key = ("wa", pc % 2)
            self.dma(wa[:, :, :], self.wada_d[li, :, :, pc * 128:(pc + 1) * 128], [], [key], q="pool")
            b = bank_fn() if bank_fn is not None else self.bank()
            for k in range(8):
                self.mm(self.pb(b, 2), wa[:, k, :], self.CACT[:, k, :], k == 0, k == 7, [key, "cact"], [("ps", b)])
            self.ts(self.MOD[:, li, pc, :], self.pb(b, 2), self.BADA[:, li, pc:pc + 1], None, ALU.add, None, ["const"],
                    [("ps", b), ("modp", li, pc // 8)])

    def mod_ab(self, li, wi):
        NW, jsh, jsc = ((self.N1, 0, 8), (self.N2, 24, 32))[wi]
        for n in range(2):
            self.stt(self.AB[:, wi, 0, :, n], self.MOD[:, li, jsc:jsc + 8, n], 1.0, NW[:, li, :], ALU.add, ALU.mult,
                     [("modp", li, jsc // 8), "const"], ["ab"])
            self.copy(self.AB[:, wi, 1, :, n], self.MOD[:, li, jsh:jsh + 8, n], [("modp", li, jsh // 8)], ["ab"])

    def phase_mod(self, li):
        if li == 0:
            self.mod_pump(0, 16)
        else:
            self.mod_pump(li, 48)
            self.mod_ab(li, 1)
        self.mod_ab(li, 0)
        lam_init = 0.8 - 0.6 * math.exp(-0.3 * li)
        self.lam_init = lam_init
        t0 = self.STG[2][:, 0:128]
        s12 = self.SM[:, 0:2]
        for i in range(2):
            self.tt(t0[:, i * 64:(i + 1) * 64], self.LAMQ[:, li, i * 128:i * 128 + 64],
                    self.LAMQ[:, li, i * 128 + 64:i * 128 + 128], ALU.mult, ["const"], [("stg", 2)])
        self.S.op("dve", lambda e: e.reduce_sum(out=s12, in_=t0.rearrange("p (i d) -> p i d", i=2), axis=AX.X),
                  [("stg", 2)], ["sm"])
        self.act(s12, s12, AF.Exp, [], ["sm"])
        self.stt(self.SM[:, 2:3], self.SM[:, 1:2], -lam_init, self.SM[:, 0:1], ALU.add, ALU.subtract, [], ["sm"])
        self.ts(self.SM[:, 3:4], self.SUBW[:, li:li + 1], 1.0 - lam_init, None, ALU.mult, None, ["const"], ["sm"])
        oml = self.SM[:, 4:8].rearrange("p (a b) -> p a b", a=2)
        self.ts(oml, self.LB[:, li, :, :], -1.0, 1.0, ALU.mult, ALU.add, ["lb"], ["sm"])

    def rstd_bcast(self, xv, w, RS, rskey, xkeys, sqbuf, sqkey):
        self.act(sqbuf[:, :, 0:w], xv, AF.Square, xkeys, [sqkey])
        b = self.bank()
        for k in range(8):
            self.mm(self.pb(b, w), self.ONES, sqbuf[:, k, 0:w], k == 0, k == 7, [sqkey, "const"], [("ps", b)])
        self.act(RS[:, 0:w], self.pb(b, w), AF.Identity, ["const"], [("ps", b), rskey], scale=1.0 / D,
                 bias=self.EPS_T[:, 0:1])
        self.act(RS[:, 0:w], RS[:, 0:w], AF.Ln, [], [rskey])
        self.act(RS[:, 0:w], RS[:, 0:w], AF.Exp, [], [rskey], scale=-0.5)

    def norm_mod(self, tok0, w, n, wi, HT, htkey, tmp, xkey="X"):
        SQ, RS, T1 = tmp
        xv = self.X[:, :, tok0:tok0 + w]
        self.rstd_bcast(xv, w, RS, "rs", [xkey], SQ, "sq")
        for k in range(8):
            t1 = T1[k % 2]
            self.tt(t1[:, 0:w], self.X[:, k, tok0:tok0 + w], RS[:, 0:w], ALU.mult, [xkey, "rs"], [("t1", k % 2)])
            self.act(HT[:, k, 0:w], t1[:, 0:w], AF.Identity, [("t1", k % 2), "ab"], [htkey],
                     scale=self.AB[:, wi, 0, k, n:n + 1], bias=self.AB[:, wi, 1, k, n:n + 1])

    def groups(self, ctx=True):
        g = [(0, CL, 1)] if ctx else []
        for i in range(4):
            g.append((CL + 512 * i, 512, 0))
        return g

    def phase_proj(self, li):
        WIN = self.alloc([128, 8, 3072], BF16, "win")
        self.dma(self.ROPE[:, :, :], self.k_rope_d, [], ["rope"], q="pool")
        for pc in range(12):
            self.dma(WIN[:, :, pc * 256:(pc + 1) * 256], self.win_d[li, :, :, pc * 256:(pc + 1) * 256], [], [("win", pc)],
                     q="pool")
        HTs = [self.alloc([128, 8, 512], BF16, f"ht{i}") for i in range(2)]
        SQ = self.alloc([128, 8, 512], BF16, "sq")
        RS = self.alloc([128, 512], F32, "rs")
        T1 = [self.alloc([128, 512], F32, f"t1{i}") for i in range(2)]
        QS = [self.alloc([128, 512], BF16, f"qs{i}") for i in range(2)]
        TA = [self.alloc([128, 512], F32, f"ta{i}") for i in range(2)]
        TB = [self.alloc([128, 512], F32, f"tb{i}") for i in range(2)]
        rr = 0
        grps = self.groups()
        t0_, w0_, n0_ = grps[0]
        xk = (lambda n_: "X")
        self.norm_mod(t0_, w0_, n0_, 0, HTs[0], ("ht", 0), (SQ, RS, T1), xkey=xk(n0_))
        for gi, (tok0, w, n) in enumerate(grps):
            HT = HTs[gi % 2]
            hk = ("ht", gi % 2)
            if gi + 1 < len(grps):
                t1_, w1_, n1_ = grps[gi + 1]
                self.norm_mod(t1_, w1_, n1_, 0, HTs[(gi + 1) % 2], ("ht", (gi + 1) % 2), (SQ, RS, T1), xkey=xk(n1_))
            rope_pending = []
            for cc in list(range(0, 8)) + [12, 13, 14, 15, 16, 17, 20, 21, 22, 23]:
                b = self.bank()
                for k in range(8):
                    self.mm(self.pb(b, w), WIN[:, k, cc * 128:(cc + 1) * 128], HT[:, k, 0:w], k == 0, k == 7,
                            [("win", cc // 2), hk], [("ps", b)])
                while len(rope_pending) > (1 if cc < 8 else 0):
                    rope_pending.pop(0)()
                if cc < 8:
                    dst = self.QK[cc * 128:(cc + 1) * 128, tok0:tok0 + w]
                    if n == 1:
                        sg, sk = self.stage(BF16, w)
                        self.act(sg, self.pb(b, w), AF.Copy, [], [("ps", b), sk])
                        self.dma(dst, sg, [sk], ["QK"])
                    else:
                        i = rr % 2
                        rr += 1
                        lt0 = tok0 - CL
                        self.act(QS[i][:, 0:w], self.pb(b, w), AF.Copy, [], [("ps", b), ("qs", i)])
                        self.tt(TA[i][:, 0:w], self.pb(b, w), self.ROPE[:, 0, lt0:lt0 + w], ALU.mult, ["rope"],
                                [("ps", b), ("ta", i)])

                        def rope2(i=i, w=w, lt0=lt0, dst=dst):
                            b2 = self.bank()
                            self.mm(self.pb(b2, w), self.RPM, QS[i][:, 0:w], True, True, [("qs", i), "const"], [("ps", b2)])
                            self.tt(TB[i][:, 0:w], self.pb(b2, w), self.ROPE[:, 1, lt0:lt0 + w], ALU.mult, ["rope"],
                                    [("ps", b2), ("tb", i)])
                            sg, sk = self.stage(BF16, w)
                            self.tt(sg, TA[i][:, 0:w], TB[i][:, 0:w], ALU.add, [("ta", i), ("tb", i)], [sk])
                            self.dma(dst, sg, [sk], ["QK"])

                        rope_pending.append(rope2)
                elif cc >= 22:
                    sg, sk = self.stage(BF16, w)
                    self.act(sg, self.pb(b, w), AF.Copy, [], [("ps", b), sk])
                    self.dma(self.UF[(cc - 22) * 128:(cc - 21) * 128, tok0:tok0 + w], sg, [sk], ["UF"])
                else:
                    row = {12: 0, 13: 128, 14: 256, 15: 384, 16: 512, 17: 640, 20: 768, 21: 896}[cc]
                    sg, sk = self.stage(F32, w)
                    self.act(sg, self.pb(b, w), AF.Copy, [], [("ps", b), sk])
                    self.dma(self.GZ[row:row + 128, tok0:tok0 + w], sg, [sk], ["GZ"])
                if li == 0 and cc % 2 == 1:
                    self.mod_pump(0, 1)
            for t in range(w // 128):
                tg = tok0 + t * 128
                for (c0, cw, vc0) in ((1024, 512, 0), (2304, 256, 512)):
                    b = self.bank()
                    for k in range(8):
                        self.mm(self.pb(b, cw), HT[:, k, t * 128:(t + 1) * 128], WIN[:, k, c0:c0 + cw], k == 0, k == 7,
                                [("win", c0 // 256), ("win", (c0 + cw - 1) // 256), hk], [("ps", b)])
                    sg, sk = self.stage(BF16, cw)
                    self.copy(sg, self.pb(b, cw), [], [("ps", b), sk])
                    self.dma(self.VT[tg:tg + 128, vc0:vc0 + cw], sg, [sk], ["VT"])
        if li == 0:
            self.mod_pump(0, 48)
            self.mod_ab(0, 1)

    def phase_hgrn(self, li):
        A = self.alloc
        G = [A([128, TT], F32, f"g{i}") for i in range(3)]
        QH = A([128, TT], F32, "qh")
        QTL = [A([128, TT], BF16, f"qtl{i}") for i in range(2)]
        KTL = [A([128, TT], BF16, f"ktl{i}") for i in range(2)]
        KTM = [A([128, NT, 128], BF16, f"ktm{i}") for i in range(2)]
        VTM = A([128, NT, 128], BF16, "vtm")
        ST = [A([128, NCH, 128], BF16, f"st{i}") for i in range(2)]
        SS = [A([128, 128], F32, f"ss{i}") for i in range(2)]
        TMPS = [[A([128, 128], F32, f"tmps{d}{i}") for i in range(2)] for d in range(2)]
        TMPS2 = [A([128, 128], F32, f"ssb{d}") for d in range(2)]
        EE = [A([128, 5, NCH], F32, f"ee{i}") for i in range(2)]
        save = self.off
        self.off = self.offs["rope"]
        KX = [A([128, 3, 128], BF16, f"kx{i}") for i in range(6)]
        AM = [A([128, 2, 128], BF16, f"am{i}") for i in range(4)]
        SQ = A([64, 512], BF16, "osq")
        assert self.off <= self.offs["rope"] + 8192
        self.off = save
        V12T = [A([128, 512], BF16, f"v12t{i}") for i in range(2)]
        SQs = [SQ, A([64, 512], BF16, "osq1")]
        OGs = [G[1], QH]
        MASK2 = self.MATS[:, 3:5, :]
        self.stg_limit = 2
        t_lo = 2 if self.last else 0
        for hp in range(2):
            self.fnet_begin(hp, V12T)
            first_loads = [True]
            for d in range(2):
                g0, g1, g2 = G
                ee = EE[d]
                HW, HC = TT // 2, NCH // 2
                H = range(2)

                def hs(buf, hh):
                    return buf[:, hh * HW:(hh + 1) * HW]

                def hv(buf, hh):
                    return buf[:, hh * HW:(hh + 1) * HW].rearrange("p (c i) -> p c i", i=64)

                def es(k, hh):
                    return ee[:, k, hh * HC:(hh + 1) * HC]

                def eb(k, hh):
                    return ee[:, k, hh * HC:(hh + 1) * HC].unsqueeze(2).to_broadcast([128, HC, 64])

                r_lo = 256 + 256 * d + hp * 128
                for hh in H:
                    self.dma(hs(g0, hh), self.GZ[r_lo:r_lo + 128, hh * HW:(hh + 1) * HW], ["GZ"], [("g0", hh)])
                if first_loads[0]:
                    first_loads[0] = False
                    self.dma(QH[:, :], self.GZ[hp * 128:(hp + 1) * 128, :], ["GZ"], ["qh"])
                    self.dma(VTM[:, :, :],
                             self.VT[:, 512 + hp * 128:512 + (hp + 1) * 128].rearrange("(t p) c -> p t c", p=128),
                             ["VT"], ["vtm"])
                for hh in H:
                    self.act(hs(g0, hh), hs(g0, hh), AF.Sigmoid, [], [("g0", hh)])
                for hh in H:
                    self.act(hs(g0, hh), hs(g0, hh), AF.Identity, ["sm", "lb"], [("g0", hh)],
                             scale=self.SM[:, 4 + 2 * d + hp:5 + 2 * d + hp], bias=self.LB[:, li, d, hp:hp + 1])
                for hh in H:
                    self.act(hs(g1, hh), hs(g0, hh), AF.Ln, [("g0", hh)], [("g1", hh)])
                self.fnet_step()
                for hh in H:
                    self.S.op("dve", lambda e, o=hs(g2, hh), a=hs(self.SCANM, hh), bb=hs(g1, hh): e.tensor_tensor_scan(
                        out=o, data0=a, data1=bb, initial=0.0, op0=ALU.mult, op1=ALU.add), [("g1", hh), "const"],
                        [("g2", hh)])
                    self.copy(es(0, hh), hv(g2, hh)[:, :, 63], [("g2", hh)], [("ee", d, hh)])
                    if d == 1:
                        self.tt(hv(g2, hh), eb(0, hh), hv(g2, hh), ALU.subtract, [("ee", d, hh)], [("g2", hh)])
                        self.tt(hs(g2, hh), hs(g2, hh), hs(g1, hh), ALU.add, [("g1", hh)], [("g2", hh)])
                    mid = 31 if d == 0 else 32
                    self.copy(es(1, hh), hv(g2, hh)[:, :, mid], [("g2", hh)], [("ee", d, hh)])
                    self.tt(hv(g2, hh), hv(g2, hh), eb(1, hh), ALU.subtract, [("ee", d, hh)], [("g2", hh)])
                self.fnet_step()
                for hh in H:
                    self.act(hs(g1, hh), hs(g2, hh), AF.Exp, [("g2", hh)], [("g1", hh)], scale=-1.0)
                    self.act(hs(g2, hh), hs(g2, hh), AF.Exp, [], [("g2", hh)])
                self.fnet_step()
                for hh in H:
                    self.tt(hs(QTL[d], hh), hs(QH, hh), hs(g2, hh), ALU.mult, ["qh", ("g2", hh)], [("qtl", d, hh)])
                    self.act(hs(g0, hh), hs(g0, hh), AF.Identity, ["const"], [("g0", hh)], scale=-1.0,
                             bias=self.ONE_T[:, 0:1])
                    self.tt(hs(KTL[d], hh), hs(g0, hh), hs(g1, hh), ALU.mult, [("g0", hh), ("g1", hh)], [("ktl", d, hh)])
                    self.act(es(2, hh), es(0, hh), AF.Exp, [], [("ee", d, hh)])
                    self.tt(es(3, hh), es(0, hh), es(1, hh), ALU.subtract, [], [("ee", d, hh)])
                    self.act(es(3, hh), es(3, hh), AF.Exp, [], [("ee", d, hh)])
                    self.act(es(4, hh), es(1, hh), AF.Exp, [], [("ee", d, hh)])
                self.fnet_step()
                K2 = g1[:, 0:TT // 2].bitcast(BF16)
                for hh in H:
                    self.tt(hv(K2, hh), hv(KTL[d], hh), eb(3, hh), ALU.mult,
                            [("ktl", d, 0), ("ktl", d, 1), ("ee", d, hh)], [("g1", 0), ("k2", hh)])
                for t4 in range(0, NT, 4):
                    b = self.bank(0, 4)
                    nn = min(4, NT - t4)
                    pbb = self.pb(b).bitcast(BF16)
                    for i in range(nn):
                        t = t4 + i
                        self.tr(pbb[:, i * 128:(i + 1) * 128], K2[:, t * 128:(t + 1) * 128], [("g1", 0), ("k2", t // 9)],
                                [("ps", b)])
                    self.copy(KTM[d][:, t4:t4 + nn, :], pbb[:, 0:nn * 128].rearrange("p (t c) -> p t c", c=128), [],
                              [("ps", b), ("ktm", d)])
            orders = [list(range(NCH)), [3, 2, 1, 0] + list(range(NCH - 1, 3, -1))]
            SS2 = [[SS[d], TMPS2[d]] for d in range(2)]
            for d in range(2):
                self.memset(SS2[d][0][:, :], 0.0, [("ss", d, 0)])
            for n8 in range(0, NCH, 8):
                self.fnet_step()
                kvb = [{}, {}]
                for d in range(2):
                    chunk = orders[d][n8:n8 + 8]
                    bAB = (self.bank(0, 4), self.bank(0, 4))
                    cntb = [0, 0]
                    for n in chunk:
                        t, j = n // 2, n % 2
                        b = bAB[j]
                        i = cntb[j]
                        cntb[j] += 1
                        self.mm(self.pb(b)[:, i * 128:(i + 1) * 128], KTM[d][64 * j:64 * j + 64, t, :],
                                VTM[64 * j:64 * j + 64, t, :], True, True, [("ktm", d), "vtm"], [("ps", b)])
                        kvb[d][n] = (b, i)
                for ci in range(len(orders[0][n8:n8 + 8])):
                    for d in range(2):
                        n = orders[d][n8 + ci]
                        ee = EE[d]
                        b, i = kvb[d][n]
                        step = n8 + ci
                        s_cur, s_nxt = SS2[d][step % 2], SS2[d][(step + 1) % 2]
                        k_cur, k_nxt = ("ss", d, step % 2), ("ss", d, (step + 1) % 2)
                        self.act(ST[d][:, n, :], s_cur[:, :], AF.Identity, [k_cur, ("ee", d, 0), ("ee", d, 1)], [("st", d)],
                                 scale=ee[:, 4, n:n + 1], bias=self.ZERO_T[:, 0:1])
                        self.stt(s_nxt[:, :], s_cur[:, :], ee[:, 2, n:n + 1], self.pb(b)[:, i * 128:(i + 1) * 128], ALU.mult,
                                 ALU.add, [k_cur, ("ee", d, 0), ("ee", d, 1)], [("ps", b), k_nxt])
            self.fnet_finish()
            self.S.barrier()
            for hl in range(2):
                h = 2 * hp + hl
                og = OGs[hl]
                self.dma(og[0:64, :], self.GZ[768 + 64 * h:768 + 64 * h + 64, :], ["GZ"], [("og", hl)])
                self.act(og[0:64, :], og[0:64, :], AF.Silu, [], [("og", hl)])
            tiles = list(range(t_lo, NT))
            st = {"kx": 0, "am": 0, "ab": 0}

            def stage_kx(t):
                kxs = []
                for d in range(2):
                    kx = KX[st["kx"] % 6]
                    kk_ = ("kx", st["kx"] % 6)
                    st["kx"] += 1
                    H1 = self.HA if d == 0 else self.HB
                    H2 = self.HB if d == 0 else self.HA
                    kfull = KTL[d][:, t * 128:(t + 1) * 128]
                    qfull = QTL[d][:, t * 128:(t + 1) * 128]
                    self.tt(kx[:, 0, :], kfull, H1, ALU.mult, ["const"], [kk_])
                    self.tt(kx[:, 1, :], kfull, H2, ALU.mult, ["const"], [kk_])
                    self.tt(kx[:, 2, :], qfull, H2, ALU.mult, ["const"], [kk_])
                    kxs.append((kx, kk_))
                return kxs

            def stage_a(t, kxs):
                res = []
                for hl in range(2):
                    r0 = 64 * hl
                    ba = 4 + st["ab"] % 4
                    st["ab"] += 1
                    for d in range(2):
                        kx, kk_ = kxs[d]
                        po = self.pb(ba)[:, d * 128:(d + 1) * 128]
                        self.mm(po, kx[r0:r0 + 64, 0, :], QTL[d][r0:r0 + 64, t * 128:(t + 1) * 128], True, False, [kk_],
                                [("ps", ba)])
                        self.mm(po, kx[r0:r0 + 64, 1, :], kx[r0:r0 + 64, 2, :], False, True, [kk_], [("ps", ba)])
                    am = AM[st["am"] % 4]
                    ak = ("am", st["am"] % 4)
                    st["am"] += 1
                    self.tt(am[:, :, :], self.pb(ba, 256).rearrange("p (d c) -> p d c", d=2), MASK2, ALU.mult, ["const"],
                            [("ps", ba), ak])
                    res.append((am, ak))
                return res

            def stage_b(t, res, gi, i, nn):
                for hl in range(2):
                    r0 = 64 * hl
                    bo = 2 * (gi % 2) + hl
                    am, ak = res[hl]
                    po = self.ps[0:64, bo * 512 + i * 128:bo * 512 + (i + 1) * 128]
                    for d in range(2):
                        self.mm(po, VTM[:, t, r0:r0 + 64], am[:, d, :], d == 0, False, ["vtm", ak], [("ps", bo)])
                    for j in range(2):
                        n = 2 * t + j
                        for d in range(2):
                            self.mm(po[:, 64 * j:64 * j + 64], ST[d][r0:r0 + 64, n, r0:r0 + 64],
                                    QTL[d][r0:r0 + 64, n * 64:(n + 1) * 64], False, (j == 1 and d == 1), [], [("ps", bo)])
                if i == nn - 1:
                    w = nn * 128
                    tok0 = (t - nn + 1) * 128
                    for hl in range(2):
                        bo = 2 * (gi % 2) + hl
                        OB = G[2][0:64, 512 * hl:512 * hl + 512]
                        obk = ("ob", hl)
                        pov = self.ps[0:64, bo * 512:bo * 512 + w]
                        self.act(OB[:, 0:w], pov, AF.Copy, [], [("ps", bo), obk])
                        self.act(SQs[hl][:, 0:w], OB[:, 0:w], AF.Square, [obk], [("osq", hl)])

                    def part2(w=w, tok0=tok0):
                        for hl in range(2):
                            h = 2 * hp + hl
                            OB = G[2][0:64, 512 * hl:512 * hl + 512]
                            RSO = G[0][0:64, 512 * hl:512 * hl + 512]
                            obk, rsk = ("ob", hl), ("rso", hl)
                            bn = 4 + st["ab"] % 4
                            st["ab"] += 1
                            self.mm(self.ps[0:64, bn * 512:bn * 512 + w], self.MATS[0:64, 2, 0:64], SQs[hl][:, 0:w], True, True,
                                    [("osq", hl), "const"], [("ps", bn)])
                            self.act(RSO[:, 0:w], self.ps[0:64, bn * 512:bn * 512 + w], AF.Identity, ["const"],
                                     [("ps", bn), rsk], scale=1.0 / 64, bias=self.EPS_T[0:64, 0:1])
                            self.act(RSO[:, 0:w], RSO[:, 0:w], AF.Ln, [], [rsk])
                            self.act(RSO[:, 0:w], RSO[:, 0:w], AF.Exp, [], [rsk], scale=-0.5)
                            self.tt(OB[:, 0:w], OB[:, 0:w], RSO[:, 0:w], ALU.mult, [rsk], [obk])
                            sg, sk = self.stage(BF16, 512, 64)
                            self.stt(sg[:, 0:w], OB[:, 0:w], self.HNW[0:64, li:li + 1], OGs[hl][0:64, tok0:tok0 + w], ALU.mult,
                                     ALU.mult, [obk, ("og", hl), "const"], [sk])
                            self.dma(self.MIX[512 + 64 * h:512 + 64 * h + 64, tok0:tok0 + w], sg[:, 0:w], [sk], ["MIX"])

                    ep_pending.append(part2)

            sched = []
            for gi, t4 in enumerate(range(t_lo, NT, 4)):
                nn = min(4, NT - t4)
                for i in range(nn):
                    sched.append((t4 + i, gi, i, nn))
            ns = len(sched)
            kxq = [stage_kx(sched[0][0])]
            if ns > 1:
                kxq.append(stage_kx(sched[1][0]))
            nxt = stage_a(sched[0][0], kxq.pop(0))
            ep_pending = []
            for idx, (t, gi, i, nn) in enumerate(sched):
                cur = nxt
                if idx + 2 < ns:
                    kxq.append(stage_kx(sched[idx + 2][0]))
                nxt = stage_a(sched[idx + 1][0], kxq.pop(0)) if idx + 1 < ns else None
                run_now = list(ep_pending)
                ep_pending.clear()
                stage_b(t, cur, gi, i, nn)
                for f_ in run_now:
                    f_()
            for f_ in ep_pending:
                f_()
            self.S.barrier()
        self.stg_limit = 6

    def phase_fnet_a(self, li):
        A = self.alloc
        WF = A([128, 2, 256], BF16, "wf")
        WCS = A([128, 2, 512], BF16, "wcs")
        UFT = A([128, 2, TT], BF16, "uft")
        V12C = A([128, 2, 512], BF16, "v12c")
        self.dma(WF[:, :, :], self.wf_d[li], [], ["wf"], q="pool")
        self.dma(UFT[:, :, :], self.UF.rearrange("(k p) t -> p k t", p=128), ["UF"], ["uft"])
        for which in range(2):
            for mo in range(2):
                b = self.bank()
                for kc in range(2):
                    self.mm(self.pb(b, 256), self.DFT64[:, which, kc, mo * 128:(mo + 1) * 128], WF[:, kc, :], kc == 0,
                            kc == 1, ["const", "wf"], [("ps", b)])
                self.copy(WCS[:, mo, which * 256:(which + 1) * 256], self.pb(b, 256), [], [("ps", b), "wcs"])
        t_lo = 2 if self.last else 0
        for t in range(t_lo, NT):
            b = self.bank()
            for kc in range(2):
                self.mm(self.pb(b), UFT[:, kc, t * 128:(t + 1) * 128], WCS[:, kc, :], kc == 0, kc == 1, ["uft", "wcs"],
                        [("ps", b)])
            if t < 2:
                self.copy(V12C[:, t, :], self.pb(b), [], [("ps", b), "v12c"])
            else:
                sg, sk = self.stage(BF16, 512)
                if t % 2 == 0:
                    self.copy(sg, self.pb(b), [], [("ps", b), sk])
                else:
                    self.act(sg, self.pb(b), AF.Copy, [], [("ps", b), sk])
                self.dma(self.V12D[t * 128:(t + 1) * 128, :], sg, [sk], ["V12D"])
        if not self.last:
            for m in range(2):
                b = self.bank()
                for tt_ in range(2):
                    for which in range(2):
                        self.mm(self.pb(b, CL), V12C[:, tt_, which * 256 + m * 128:which * 256 + (m + 1) * 128],
                                self.DFTC[:, which, tt_, :], tt_ == 0 and which == 0, tt_ == 1 and which == 1,
                                ["v12c", "const"], [("ps", b)])
                sg, sk = self.stage(BF16, CL)
                self.copy(sg, self.pb(b, CL), [], [("ps", b), sk])
                self.dma(self.MIX[768 + m * 128:768 + (m + 1) * 128, 0:CL], sg, [sk], ["MIX"])

    def fnet_begin(self, half, V12T):
        self.fn = {"half": half, "dma": 0, "mm": 0, "V12T": V12T}
        self.fnet_dma()

    def fnet_dma(self):
        f = self.fn
        tt_ = f["dma"]
        if tt_ >= 16:
            return
        f["dma"] += 1
        i = tt_ % 2
        for which in range(2):
            db = self.STG[2 + 2 * which + i][:, :].bitcast(BF16)
            self.dma(db, self.k_dftL_d[which, :, tt_, f["half"] * 1024:(f["half"] + 1) * 1024], [],
                     [("fdb", which, i)], q="pool")
        self.dma(f["V12T"][i][:, :], self.V12D[(2 + tt_) * 128:(3 + tt_) * 128, :], ["V12D"], [("v12t", i)])

    def fnet_step(self):
        f = self.fn
        tt_ = f["mm"]
        if tt_ >= 16:
            return
        f["mm"] += 1
        self.fnet_dma()
        i = tt_ % 2
        v12 = f["V12T"][i]
        for m in range(2):
            for tc in range(2):
                b = 4 + m * 2 + tc
                for which in range(2):
                    db = self.STG[2 + 2 * which + i][:, :].bitcast(BF16)
                    self.mm(self.pb(b), v12[:, which * 256 + m * 128:which * 256 + (m + 1) * 128],
                            db[:, tc * 512:(tc + 1) * 512], tt_ == 0 and which == 0, tt_ == 15 and which == 1,
                            [("v12t", i), ("fdb", which, i)], [("ps", b)])
        if tt_ == 15:
            for m in range(2):
                for tc in range(2):
                    b = 4 + m * 2 + tc
                    sg, sk = self.stage(BF16, 512)
                    self.copy(sg, self.pb(b), [], [("ps", b), sk])
                    c0 = CL + f["half"] * 1024 + tc * 512
                    self.dma(self.MIX[768 + m * 128:768 + (m + 1) * 128, c0:c0 + 512], sg, [sk], ["MIX"])

    def fnet_finish(self):
        while self.fn["mm"] < 16:
            self.fnet_step()

    def phase_att(self, li):
        A = self.alloc
        QT = [A([128, TT], BF16, f"qt{i}") for i in range(2)]
        KT = [A([128, TT], BF16, f"kt{i}") for i in range(2)]
        VH = [A([128, NT, 128], BF16, f"vh{i}") for i in range(2)]
        NPB = 8
        PB = [A([128, 512], BF16, f"pb{i}") for i in range(NPB)]
        EV = [[A([128, 512], F32, f"ev{i}{j}") for j in range(4)] for i in range(2)]
        DD = A([128, 512], F32, "dd")
        SQ = A([128, 512], BF16, "asq")
        RS = A([128, 512], F32, "ars")
        items = []
        jobs = []
        for h in range(4):
            i = h % 2
            jl = [(CL + 512 * qc, 512, 0, NT) for qc in range(4)]
            if not self.last:
                jl.append((0, CL, 0, 2))
            for (q0, w, kt0, kt1) in jl:
                jobs.append((h, q0, w, kt0, kt1))
                for kti in range(kt0, kt1):
                    items.append((len(jobs) - 1, kti))
        loaded = set()
        state = {"pr": 0, "sb": 0}

        def load_head(h):
            if h in loaded or h >= 4:
                return
            loaded.add(h)
            i = h % 2
            self.dma(QT[i][:, :], self.QK[h * 128:(h + 1) * 128, :], ["QK"], [("qt", i)])
            self.dma(KT[i][:, :], self.QK[512 + h * 128:512 + (h + 1) * 128, :], ["QK"], [("kt", i)])
            self.dma(VH[i][:, :, :], self.VT[:, h * 128:(h + 1) * 128].rearrange("(t p) c -> p t c", p=128), ["VT"],
                     [("vh", i)])

        def stage_s(item):
            jb, kti = item
            h, q0, w, kt0, kt1 = jobs[jb]
            load_head(h)
            i = h % 2
            out = []
            bss = []
            for c in range(2):
                bs = 4 + state["sb"] % 4
                state["sb"] += 1
                bss.append(bs)
                self.mm(self.pb(bs, w), KT[i][64 * c:64 * c + 64, kti * 128:(kti + 1) * 128],
                        QT[i][64 * c:64 * c + 64, q0:q0 + w], True, True, [("kt", i), ("qt", i)], [("ps", bs)])
            for c in range(2):
                p = PB[state["pr"] % NPB]
                pk = ("pb", state["pr"] % NPB)
                state["pr"] += 1
                self.act(p[:, 0:w], self.pb(bss[c], w), AF.Exp, [], [("ps", bss[c]), pk], scale=0.125)
                out.append((p, pk))
            return out

        njob = 0
        pending = []

        def sbank():
            b = 4 + state["sb"] % 4
            state["sb"] += 1
            return b

        nxt = stage_s(items[0])
        for idx, item in enumerate(items):
            cur = nxt
            nxt = stage_s(items[idx + 1]) if idx + 1 < len(items) else None
            jb, kti = item
            h, q0, w, kt0, kt1 = jobs[jb]
            i = h % 2
            for c in range(2):
                p, pk = cur[c]
                self.mm(self.pb(2 * c, w), VH[i][:, kti, :], p[:, 0:w], kti == kt0, kti == kt1 - 1, [("vh", i), pk],
                        [("ps", 2 * c)])
                self.mm(self.pb(2 * c + 1, w), self.ONES, p[:, 0:w], kti == kt0, kti == kt1 - 1, ["const", pk],
                        [("ps", 2 * c + 1)])
            load_head(h + 1)
            if li + 1 < self.nlayers and kti % 6 == 3:
                self.mod_pump(li + 1, 1, sbank)
            if kti == kt1 - 1:
                for pe_ in pending:
                    pe_[1]()
                pending.clear()
                ev = EV[njob % 2]
                ek = [("ev", njob % 2, x) for x in range(4)]
                njob += 1
                for x in range(4):
                    self.copy(ev[x][:, 0:w], self.pb(x, w), [], [("ps", x), ek[x]])
                for x in (1, 3):
                    self.act(ev[x][:, 0:w], ev[x][:, 0:w], AF.Ln, [], [ek[x]])
                    self.act(ev[x][:, 0:w], ev[x][:, 0:w], AF.Exp, [], [ek[x]], scale=-1.0)
                for c in range(2):
                    self.tt(ev[2 * c][:, 0:w], ev[2 * c][:, 0:w], ev[2 * c + 1][:, 0:w], ALU.mult, [ek[2 * c + 1]],
                            [ek[2 * c]])
                self.stt(DD[:, 0:w], ev[2][:, 0:w], self.SM[:, 2:3], ev[0][:, 0:w], ALU.mult, ALU.add,
                         [ek[0], ek[2], "sm"], ["dd"])
                self.tt(SQ[:, 0:w], DD[:, 0:w], DD[:, 0:w], ALU.mult, ["dd"], ["asq"])

                def part2(h=h, q0=q0, w=w):
                    bn = 4 + state["sb"] % 4
                    state["sb"] += 1
                    self.mm(self.pb(bn, w), self.ONES, SQ[:, 0:w], True, True, ["asq", "const"], [("ps", bn)])
                    self.act(RS[:, 0:w], self.pb(bn, w), AF.Identity, ["const"], [("ps", bn), "ars"], scale=1.0 / 128,
                             bias=self.EPS_T[:, 0:1])
                    self.act(RS[:, 0:w], RS[:, 0:w], AF.Ln, [], ["ars"])
                    self.act(RS[:, 0:w], RS[:, 0:w], AF.Exp, [], ["ars"], scale=-0.5)
                    sg, sk = self.stage(BF16, w)
                    self.stt(sg, DD[:, 0:w], self.SM[:, 3:4], RS[:, 0:w], ALU.mult, ALU.mult, ["dd", "ars", "sm"], [sk])
                    self.dma(self.MIX[h * 128:(h + 1) * 128, q0:q0 + w], sg, [sk], ["MIX"])

                pending.append([6, part2])
            for pe_ in list(pending):
                pe_[0] -= 1
                if pe_[0] <= 0:
                    pe_[1]()
                    pending.remove(pe_)
        for pe_ in pending:
            pe_[1]()

    def phase_wout(self, li):
        grps = self.groups(ctx=not self.last)
        T_ = sum(g[1] for g in grps)
        H2 = self.alloc([128, 8, T_], BF16, "h2")
        self.H2 = H2
        self.H2_end = self.off
        SQ = self.alloc([128, 8, 512], BF16, "sq")
        RS = self.alloc([128, 512], F32, "rs")
        T1 = [self.alloc([128, 512], F32, f"t1{i}") for i in range(2)]
        WO = self.alloc([128, 8, D], BF16, "wo")
        for pc in range(4):
            self.dma(WO[:, :, pc * 256:(pc + 1) * 256], self.wo_d[li, :, :, pc * 256:(pc + 1) * 256], [], [("wo", pc)],
                     q="pool")
        MG = [self.alloc([128, 8, 512], BF16, f"mg{i}") for i in range(2)]
        hoff = 0
        norm_pending = []
        wu0 = None
        for gi, (tok0, w, n) in enumerate(grps):
            mg = MG[gi % 2]
            mk = ("mg", gi % 2)
            self.dma(mg[:, :, 0:w], self.MIX[:, tok0:tok0 + w].rearrange("(k p) t -> p k t", p=128), ["MIX"], [mk])
            for m in range(8):
                b = self.bank()
                for k in range(8):
                    self.mm(self.pb(b, w), WO[:, k, m * 128:(m + 1) * 128], mg[:, k, 0:w], k == 0, k == 7,
                            [("wo", m // 2), mk], [("ps", b)])
                self.stt(self.X[:, m, tok0:tok0 + w], self.pb(b, w), self.MOD[:, li, 16 + m, n:n + 1],
                         self.X[:, m, tok0:tok0 + w], ALU.mult, ALU.add, ["mod"], [("ps", b), ("X", gi)])
            for f_ in norm_pending:
                f_()
            norm_pending.clear()
            norm_pending.append(lambda tok0=tok0, w=w, n=n, hoff=hoff, gi=gi: self.norm_mod(
                tok0, w, n, 1, H2[:, :, hoff:hoff + w], "h2", (SQ, RS, T1), xkey=("X", gi)))
            hoff += w
        save = self.off
        self.off = self.offs["stg0"]
        self.WU = [self.alloc([128, 8, 256], BF16, f"wu{i}") for i in range(2)]
        assert self.off <= self.offs["stg0"] + 6 * 2048
        self.off = save
        wu0 = self.WU[0]
        self.dma(wu0[:, :, 0:128], self.wu_d[li, :, :, 0:128], [], [("wupre", 0)], q="pool")
        self.dma(wu0[:, :, 128:256], self.wu_d[li, :, :, DFF:DFF + 128], [], [("wupre", 0)], q="pool")
        for f_ in norm_pending:
            f_()

    def phase_ffn(self, li):
        A = self.alloc
        if self.last:
            chunks = [(CL + 512 * i, 512, 0, 512 * i) for i in range(4)]
            TC = L
        else:
            chunks = [(0, CL, 1, 0)] + [(CL + 512 * i, 512, 0, CL + 2 + 512 * i) for i in range(4)]
            TC = TT + 2
        gsz = [4, 4, 4, 4, 3, 3]
        T_ = sum(c[1] for c in chunks)
        H2 = self.H2
        self.off = self.H2_end
        hoff = []
        o = 0
        for (x0, w, n, co) in chunks:
            hoff.append(o)
            o += w
        ACTB = A([128, 4, TC], BF16, "actb")
        CC = [A([128, TC], F32, f"cc{i}") for i in range(2)]
        UU = [A([128, TC + 2], F32, f"uu{i}") for i in range(2)]
        WD = [A([128, 4, 128], BF16, f"wd{i}") for i in range(2)]
        WU = self.WU
        for i in range(2):
            self.memset(UU[i][:, :], 0.0, [("uu", i)])
        wr = [0]

        def pair_a(j):
            wu = WU[j % 2]
            wk = ("wu", j % 2)
            if j > 0:
                self.dma(wu[:, :, 0:128], self.wu_d[li, :, :, j * 128:(j + 1) * 128], [], [wk], q="pool")
                self.dma(wu[:, :, 128:256], self.wu_d[li, :, :, DFF + j * 128:DFF + (j + 1) * 128], [], [wk], q="pool")
            for part in range(2):
                ci = j if part == 0 else NPAIR + j
                cc = CC[part]
                ck = ("cc", part)
                uu = UU[part]
                uk = ("uu", part)
                for cidx, (x0, w, n, co) in enumerate(chunks):
                    b = self.bank()
                    for k in range(8):
                        self.mm(self.pb(b, w), wu[:, k, part * 128:(part + 1) * 128],
                                H2[:, k, hoff[cidx]:hoff[cidx] + w], k == 0, k == 7, [wk, "h2"], [("ps", b)])
                    self.act(uu[:, 1 + co:1 + co + w], self.pb(b, w), AF.Copy, [], [("ps", b), uk])
                self.act(cc[:, :], uu[:, 1:TC + 1], AF.Identity, ["const", uk], [ck], scale=self.CW[:, li, ci, 1:2],
                         bias=self.CB[:, li, ci:ci + 1])
                self.stt(cc[:, :], uu[:, 0:TC], self.CW[:, li, ci, 0:1], cc[:, :], ALU.mult, ALU.add,
                         ["const", uk], [ck])
                self.stt(cc[:, :], uu[:, 2:TC + 2], self.CW[:, li, ci, 2:3], cc[:, :], ALU.mult, ALU.add,
                         ["const", uk], [ck])
            self.act(CC[0][:, :], CC[0][:, :], AF.Silu, [], [("cc", 0)])

        def pair_b(jj):
            self.tt(ACTB[:, jj, :], CC[0][:, :], CC[1][:, :], ALU.mult, [("cc", 0), ("cc", 1)], [("actb", jj)])

        def down(jbase, gs):
            for m in range(8):
                wd = WD[wr[0] % 2]
                dk = ("wd", wr[0] % 2)
                wr[0] += 1
                self.dma(wd[:, 0:gs, :], self.wd_d[li, :, jbase:jbase + gs, m * 128:(m + 1) * 128], [], [dk], q="pool")
                for (x0, w, n, co) in chunks:
                    b = self.bank()
                    for jj in range(gs):
                        self.mm(self.pb(b, w), wd[:, jj, :], ACTB[:, jj, co:co + w], jj == 0, jj == gs - 1,
                                [dk, ("actb", jj)], [("ps", b)])
                    xs = self.X[:, m, x0:x0 + w]
                    self.stt(xs, self.pb(b, w), self.MOD[:, li, 40 + m, n:n + 1], xs, ALU.mult, ALU.add, ["mod"],
                             [("ps", b), "X"])

        def down_final(jbase, gs):
            WDL = A([128, 8, gs, 128], BF16, "wdl")
            for m in range(8):
                self.dma(WDL[:, m, :, :], self.wd_d[li, :, jbase:jbase + gs, m * 128:(m + 1) * 128], [], [("wdl", m)],
                         q="pool")
            SQv = UU[0][:, 0:2048].bitcast(BF16).rearrange("p (k t) -> p k t", k=8)
            RSv = UU[1][:, 0:512]
            pend = []

            def fin(ci, x0):
                self.rstd_bcast(self.X[:, :, x0:x0 + 512], 512, RSv, ("uu", 1), [("Xf", ci)], SQv, ("uu", 0))
                for k in range(8):
                    i = 4 + k % 2
                    sg = self.STG[i][:, 0:512]
                    self.stt(sg, self.X[:, k, x0:x0 + 512], self.FNW[:, k:k + 1], RSv, ALU.mult, ALU.mult,
                             [("Xf", ci), ("uu", 1), "const"], [("stg", i)])
                    self.dma(self.outT_d[:, k, x0 - CL:x0 - CL + 512], sg, [("stg", i)], ["outT"])

            for ci, (x0, w, n, co) in enumerate(chunks):
                for m in range(8):
                    b = self.bank()
                    for jj in range(gs):
                        self.mm(self.pb(b, w), WDL[:, m, jj, :], ACTB[:, jj, co:co + w], jj == 0, jj == gs - 1,
                                [("wdl", m), ("actb", jj)], [("ps", b)])
                    xs = self.X[:, m, x0:x0 + w]
                    self.stt(xs, self.pb(b, w), self.MOD[:, li, 40 + m, n:n + 1], xs, ALU.mult, ALU.add, ["mod"],
                             [("ps", b), ("Xf", ci)])
                for f_ in pend:
                    f_()
                pend = [lambda ci=ci, x0=x0: fin(ci, x0)]
            for f_ in pend:
                f_()
            self.final_fused = True

        j = 0
        prev = None
        for grp in range(len(gsz)):
            jbase = j
            for jj in range(gsz[grp]):
                pair_a(j)
                if jj == 0 and prev is not None:
                    down(*prev)
                    prev = None
                pair_b(jj)
                j += 1
            prev = (jbase, gsz[grp])
        if self.last and self.stop is None and not self.dbg:
            down_final(*prev)
        else:
            down(*prev)
        self.S.barrier()

    def final_norm(self):
        self.off = self.persist_mark
        SQ = self.alloc([128, 8, 512], BF16, "sq")
        RS = self.alloc([128, 512], F32, "rs")
        OUT = [self.alloc([128, 8, 512], F32, f"out{i}") for i in range(2)]
        for g in range(4):
            tok0 = CL + 512 * g
            self.rstd_bcast(self.X[:, :, tok0:tok0 + 512], 512, RS, "rs", ["X"], SQ, "sq")
            o = OUT[g % 2]
            ok = ("out", g % 2)
            for k in range(8):
                self.stt(o[:, k, :], self.X[:, k, tok0:tok0 + 512], self.FNW[:, k:k + 1], RS[:, :], ALU.mult, ALU.mult,
                         ["X", "rs", "const"], [ok])
            self.dma(self.outT_d[:, :, 512 * g:512 * (g + 1)], o[:, :, :], [ok], ["outT"])

    def dump_x(self):
        self.S.barrier()
        self.dma(self.MODD[:, :, :, :], self.MOD[:, :, :, :], [], [])
        self.dma(self.XD, self.X[:, :, :], ["X"], [])


def _pk(a):
    k = a.shape[0] // 128
    return np.ascontiguousarray(a.reshape(k, 128, -1).transpose(1, 0, 2))


def _vec(a):
    return np.ascontiguousarray(a.reshape(-1, 128).T)


_CONST_CACHE = {}


def host_consts():
    if _CONST_CACHE:
        return _CONST_CACHE
    f32 = np.float32
    half = 32
    inv_freq = (10000.0 ** (-np.arange(0, half, 2, dtype=np.float64) / half))
    t = np.arange(L)
    pos_r = (t // 64).astype(np.float64)
    pos_c = (t % 64).astype(np.float64)
    ang_r = pos_r[:, None] * inv_freq
    ang_c = pos_c[:, None] * inv_freq
    ang = np.concatenate([ang_r, ang_r, ang_c, ang_c], axis=-1)
    cosT = np.cos(ang).T
    sinT = np.sin(ang).T
    rope = np.zeros((128, 2, L), f32)
    rope[:, 0] = np.concatenate([cosT, cosT], 0)
    rope[:, 1] = np.concatenate([sinT, sinT], 0)
    mats = np.zeros((128, 7, 128), f32)
    mats[:, 5] = ((np.arange(128) % 64) < 32)[None, :]
    mats[:, 6] = ((np.arange(128) % 64) >= 32)[None, :]
    mats[:, 0] = np.eye(128)
    R = np.zeros((128, 128))
    for dp in range(128):
        d = dp % 64
        base = dp - d
        if d % 32 < 16:
            R[base + d + 16, dp] = -1.0
        else:
            R[base + d - 16, dp] = 1.0
    mats[:, 1] = R
    mats[:, 2] = 1.0
    s = np.arange(128)[:, None]
    tt = np.arange(128)[None, :]
    same = (s // 64) == (tt // 64)
    mats[:, 3] = (same & (s <= tt))
    mats[:, 4] = (same & (s >= tt))
    scan = np.ones((128, TT), f32)
    scan[:, ::64] = 0.0
    i64 = np.arange(64)
    c64 = np.cos(2 * np.pi * np.outer(i64, i64) / 64) / 8.0
    s64 = np.sin(2 * np.pi * np.outer(i64, i64) / 64) / 8.0
    cbd = np.kron(np.eye(4), c64)
    sbd = np.kron(np.eye(4), s64)
    dft64 = np.stack([_pk(cbd), _pk(sbd)], axis=1).astype(f32)

    def dft(n):
        idx = np.arange(n, dtype=np.int64)
        m = (np.outer(idx, idx) % n).astype(np.float64) * (2 * np.pi / n)
        return (np.cos(m) / math.sqrt(n)).astype(f32), (-np.sin(m) / math.sqrt(n)).astype(f32)

    cL, sL = dft(L)
    dftL = np.stack([_pk(cL), _pk(sL)], axis=0)
    cC, sC = dft(CL)
    dftC = np.stack([_pk(cC), _pk(sC)], axis=1)
    _CONST_CACHE.update(k_rope=rope, k_mats=mats, k_scan=scan, k_dft64=np.ascontiguousarray(dft64),
                        k_dftL=np.ascontiguousarray(dftL), k_dftC=np.ascontiguousarray(dftC))
    return _CONST_CACHE


def host_inputs(x, c, ctx, c_ctx, w_ada, b_ada, norm1_w, norm2_w, w_in, lam_qk, subln_w, lb_param, hgrn_norm_w, w_fnet,
                w_out, w_up, conv_w, conv_b, w_down, final_norm_w, cores=range(8)):
    f = lambda a: np.ascontiguousarray(np.asarray(a, dtype=np.float32))
    x, c, ctx, c_ctx = f(x), f(c), f(ctx), f(c_ctx)
    shared = dict(host_consts())
    shared["w_ada"] = np.stack([_pk(f(w_ada[l])) for l in range(2)])
    shared["b_ada"] = np.ascontiguousarray(np.stack([_vec(f(b_ada[l])) for l in range(2)], axis=1))
    shared["norm1_w"] = np.ascontiguousarray(np.stack([_vec(f(norm1_w[l])) for l in range(2)], axis=1))
    shared["norm2_w"] = np.ascontiguousarray(np.stack([_vec(f(norm2_w[l])) for l in range(2)], axis=1))
    shared["final_norm_w"] = _vec(f(final_norm_w))
    shared["w_in"] = np.stack([_pk(f(w_in[l])) for l in range(2)])
    lam = f(lam_qk).reshape(2, 256)
    shared["lam_qk"] = np.ascontiguousarray(np.broadcast_to(lam[None], (128, 2, 256)))
    shared["subln_w"] = np.ascontiguousarray(f(subln_w).T)
    lb = f(lb_param).reshape(2, 2, 2, 128)
    shared["lb_param"] = np.ascontiguousarray(lb.transpose(3, 0, 1, 2))
    hn = f(hgrn_norm_w)
    shared["hgrn_norm_w"] = np.ascontiguousarray(np.concatenate([hn, hn], axis=1).T)
    shared["w_fnet"] = np.stack([_pk(f(w_fnet[l])) for l in range(2)])
    shared["w_out"] = np.stack([_pk(f(w_out[l])) for l in range(2)])
    shared["w_up"] = np.stack([_pk(f(w_up[l])) for l in range(2)])
    cw = f(conv_w)
    shared["conv_w"] = np.ascontiguousarray(cw.reshape(2, 3, 44, 128).transpose(3, 0, 2, 1))
    shared["conv_b"] = np.ascontiguousarray(f(conv_b).reshape(2, 44, 128).transpose(2, 0, 1))
    shared["w_down"] = np.stack([_pk(f(w_down[l])) for l in range(2)])
    maps = []
    for b in cores:
        m = dict(shared)
        m["xT"] = _pk(np.ascontiguousarray(x[b].T))
        m["ctxT"] = _pk(np.ascontiguousarray(ctx[b].T))
        m["cvec"] = np.ascontiguousarray(np.stack([_vec(c[b]), _vec(c_ctx)], axis=-1))
        maps.append(m)
    return maps


_NC_CACHE = {}


def kernel(**inputs):
    maps = host_inputs(**inputs)
    if "nc" not in _NC_CACHE:
        _NC_CACHE["nc"] = Builder().build()
    res = run_bass_kernel_spmd(_NC_CACHE["nc"], maps, core_ids=list(range(8)))
    out = np.empty((8, L, D), np.float32)
    for b in range(8):
        oT = np.asarray(res.results[b]["outT"])
        out[b] = oT.transpose(2, 1, 0).reshape(L, D)
    return out
```
